# Optimizing a Trainium2 kernel written in Bass

```python
import jax, jax.numpy as jnp
from jax import lax
import numpy as np

D_MODEL = 2048
BATCH = 4
SEQ = 4096
DEPTH = 4

D_CONV = D_MODEL // 4
CONV_WIDTH = 31
N_HEADS = 16
HEAD_DIM = 64
D_ATTN = N_HEADS * HEAD_DIM
D_POOL = D_MODEL // 4
POOL_WINDOWS = (2, 4, 8, 16)
N_POOL_GROUPS = 4
POOL_GROUP = D_POOL // N_POOL_GROUPS
N_BRANCH = 3
IN_COLS = 2 * D_CONV + 3 * D_ATTN + D_POOL + N_BRANCH * D_MODEL
D_FF = ((8 * D_MODEL // 3 + 127) // 128) * 128
N_ADA = 9
Q_BLOCK = 128
EPS = 1e-6

kernel_name = "hybrid_gated_conv_stickbreak_pool_macaron"


def rmsnorm(x, g):
    xf = x.astype(jnp.float32)
    y = xf * lax.rsqrt(jnp.mean(xf * xf, axis=-1, keepdims=True) + EPS)
    return (y * g.astype(jnp.float32)).astype(x.dtype)


def layernorm(x, g, b):
    xf = x.astype(jnp.float32)
    mu = jnp.mean(xf, axis=-1, keepdims=True)
    var = jnp.mean(jnp.square(xf - mu), axis=-1, keepdims=True)
    y = (xf - mu) * lax.rsqrt(var + EPS)
    return (y * g.astype(jnp.float32) + b.astype(jnp.float32)).astype(x.dtype)


def modulate(x, g, shift, scale):
    return rmsnorm(x, g) * (1 + scale[:, None, :]) + shift[:, None, :]


def swiglu(h, w_gu, w_d):
    gate, up = jnp.split(h @ w_gu, 2, axis=-1)
    return (jax.nn.silu(gate) * up) @ w_d


def stick_breaking_attention(q, k, v):
    B, S, H, dh = q.shape
    nb = S // Q_BLOCK
    qb = q.reshape(B, nb, Q_BLOCK, H, dh).transpose(1, 0, 3, 2, 4)
    kt = k.transpose(0, 2, 1, 3).astype(jnp.float32)
    vt = v.transpose(0, 2, 1, 3)
    key_pos = jnp.arange(S)
    inv_sqrt_d = dh ** -0.5

    def one_block(args):
        q_blk, i = args
        z = jnp.einsum('bhqd,bhkd->bhqk', q_blk.astype(jnp.float32), kt) * inv_sqrt_d
        q_pos = i * Q_BLOCK + jnp.arange(Q_BLOCK)
        mask = key_pos[None, :] < q_pos[:, None]
        log_not = jnp.where(mask, jax.nn.log_sigmoid(-z), 0.0)
        after = lax.cumsum(log_not, axis=3, reverse=True) - log_not
        w = jnp.where(mask, jnp.exp(jax.nn.log_sigmoid(z) + after), 0.0)
        return jnp.einsum('bhqk,bhkd->bhqd', w.astype(vt.dtype), vt)

    out = lax.map(one_block, (qb, jnp.arange(nb)))
    return out.transpose(1, 0, 3, 2, 4).reshape(B, S, H * dh)


def causal_mean(xg, window):
    S = xg.shape[1]
    xf = xg.astype(jnp.float32)
    cs = jnp.cumsum(xf, axis=1)
    cs_shift = jnp.pad(cs, ((0, 0), (window, 0), (0, 0)))[:, :S]
    count = jnp.minimum(jnp.arange(S) + 1, window).astype(jnp.float32)
    return ((cs - cs_shift) / count[None, :, None]).astype(xg.dtype)


def token_mixer(u, w_in, conv_w, conv_b, conv_ln_g, conv_ln_b, pool_w, pool_scale,
                w_br_conv, w_br_attn, w_br_pool, w_out):
    B, S, _ = u.shape
    proj = u @ w_in
    o1 = 2 * D_CONV
    o2 = o1 + 3 * D_ATTN
    o3 = o2 + D_POOL
    conv_in, attn_in, pool_in, gate_logits = proj[..., :o1], proj[..., o1:o2], proj[..., o2:o3], proj[..., o3:]

    a, g = jnp.split(conv_in, 2, axis=-1)
    glu = a * jax.nn.sigmoid(g)
    dw = lax.conv_general_dilated(glu, conv_w[:, None, :], window_strides=(1,),
                                  padding=[(CONV_WIDTH - 1, 0)],
                                  dimension_numbers=('NWC', 'WIO', 'NWC'),
                                  feature_group_count=D_CONV) + conv_b
    y_conv = jax.nn.silu(layernorm(dw, conv_ln_g, conv_ln_b)) @ w_br_conv

    q, k, v = jnp.split(attn_in, 3, axis=-1)
    q = q.reshape(B, S, N_HEADS, HEAD_DIM)
    k = k.reshape(B, S, N_HEADS, HEAD_DIM)
    v = v.reshape(B, S, N_HEADS, HEAD_DIM)
    y_attn = stick_breaking_attention(q, k, v) @ w_br_attn

    groups = []
    for gi, win in enumerate(POOL_WINDOWS):
        pg = pool_in[..., gi * POOL_GROUP:(gi + 1) * POOL_GROUP]
        groups.append(jnp.einsum('bsc,cd->bsd', causal_mean(pg, win) - pg, pool_w[gi]))
    y_pool = (jnp.concatenate(groups, axis=-1) * pool_scale) @ w_br_pool

    gates = jax.nn.sigmoid(gate_logits).reshape(B, S, N_BRANCH, D_MODEL)
    merged = gates[:, :, 0] * y_conv + gates[:, :, 1] * y_attn + gates[:, :, 2] * y_pool
    return merged @ w_out


def setup_inputs(seed: int = 0) -> dict:
    key = jax.random.key(seed)
    ks = jax.random.split(key, 24)
    f32 = jnp.float32

    def nrm(k, shape, fan_in, mult=1.0):
        return jax.random.normal(k, shape, f32) * (mult * fan_in ** -0.5)

    return {
        "x": jax.random.normal(ks[0], (BATCH, SEQ, D_MODEL), f32),
        "c": jax.random.normal(ks[1], (BATCH, D_MODEL), f32),
        "norm_g": 1.0 + 0.1 * jax.random.normal(ks[2], (DEPTH, 3, D_MODEL), f32),
        "w_ada": nrm(ks[3], (DEPTH, D_MODEL, N_ADA * D_MODEL), D_MODEL, 0.5),
        "b_ada": 0.02 * jax.random.normal(ks[4], (DEPTH, N_ADA * D_MODEL), f32),
        "ffn1_w_gu": nrm(ks[5], (DEPTH, D_MODEL, 2 * D_FF), D_MODEL),
        "ffn1_w_d": nrm(ks[6], (DEPTH, D_FF, D_MODEL), D_FF),
        "w_in": nrm(ks[7], (DEPTH, D_MODEL, IN_COLS), D_MODEL),
        "conv_w": nrm(ks[8], (DEPTH, CONV_WIDTH, D_CONV), CONV_WIDTH),
        "conv_b": 0.02 * jax.random.normal(ks[9], (DEPTH, D_CONV), f32),
        "conv_ln_g": 1.0 + 0.1 * jax.random.normal(ks[10], (DEPTH, D_CONV), f32),
        "conv_ln_b": 0.02 * jax.random.normal(ks[11], (DEPTH, D_CONV), f32),
        "pool_w": nrm(ks[12], (DEPTH, N_POOL_GROUPS, POOL_GROUP, POOL_GROUP), POOL_GROUP),
        "pool_scale": 1.0 + 0.1 * jax.random.normal(ks[13], (DEPTH, D_POOL), f32),
        "w_br_conv": nrm(ks[14], (DEPTH, D_CONV, D_MODEL), D_CONV),
        "w_br_attn": nrm(ks[15], (DEPTH, D_ATTN, D_MODEL), D_ATTN),
        "w_br_pool": nrm(ks[16], (DEPTH, D_POOL, D_MODEL), D_POOL),
        "w_out": nrm(ks[17], (DEPTH, D_MODEL, D_MODEL), D_MODEL),
        "ffn2_w_gu": nrm(ks[18], (DEPTH, D_MODEL, 2 * D_FF), D_MODEL),
        "ffn2_w_d": nrm(ks[19], (DEPTH, D_FF, D_MODEL), D_FF),
        "final_g": 1.0 + 0.1 * jax.random.normal(ks[20], (D_MODEL,), f32),
    }


def reference(x, c, norm_g, w_ada, b_ada, ffn1_w_gu, ffn1_w_d, w_in, conv_w, conv_b, conv_ln_g, conv_ln_b,
              pool_w, pool_scale, w_br_conv, w_br_attn, w_br_pool, w_out, ffn2_w_gu, ffn2_w_d, final_g):
    c_act = jax.nn.silu(c)
    h = x
    for l in range(DEPTH):
        ada = c_act @ w_ada[l] + b_ada[l]
        sh1, sc1, g1, sh2, sc2, g2, sh3, sc3, g3 = jnp.split(ada, N_ADA, axis=-1)
        u = modulate(h, norm_g[l, 0], sh1, sc1)
        h = h + 0.5 * g1[:, None, :] * swiglu(u, ffn1_w_gu[l], ffn1_w_d[l])
        u = modulate(h, norm_g[l, 1], sh2, sc2)
        mix = token_mixer(u, w_in[l], conv_w[l], conv_b[l], conv_ln_g[l], conv_ln_b[l], pool_w[l], pool_scale[l],
                          w_br_conv[l], w_br_attn[l], w_br_pool[l], w_out[l])
        h = h + g2[:, None, :] * mix
        u = modulate(h, norm_g[l, 2], sh3, sc3)
        h = h + 0.5 * g3[:, None, :] * swiglu(u, ffn2_w_gu[l], ffn2_w_d[l])
    return rmsnorm(h, final_g)
```

```python
import numpy as np
import concourse.bass as bass
import concourse.mybir as mybir
from concourse.bass_utils import run_bass_kernel_spmd

F32 = mybir.dt.float32
F32R = mybir.dt.float32r
BF16 = mybir.dt.bfloat16
AF = mybir.ActivationFunctionType
ALU = mybir.AluOpType

D = 2048
KC = 16
T = 512
DFF = 5504
HC = 43
DCONV = 512
CW = 31
NH = 16
DATT = 1024
DPOOL = 512
INCOLS = 10752
NADA = 9
EPS = 1e-6
POOL_WINDOWS = (2, 4, 8, 16)
HALO = 30
PH = 16


class Obj:
    __slots__ = ("name", "writes", "reads")

    def __init__(self, name):
        self.name = name
        self.writes = {}
        self.reads = {}


class Prog:
    ENGS = ("pe", "act", "dve", "pool", "sp")

    def __init__(self, nc):
        self.nc = nc
        self.sems = {}
        self.lists = {e: [] for e in self.ENGS}
        self.known = {e: {} for e in self.ENGS}
        self.guards = []
        for e in self.ENGS:
            self._sem("eng_" + e)
        self.nops = 0

    def _sem(self, name):
        if name not in self.sems:
            g = self.nc.semaphore(name)
            h = g.__enter__()
            self.guards.append(g)
            self.sems[name] = [h, 0]
        return self.sems[name]

    def op(self, eng, fn, reads=(), writes=(), pwrites=(), dma=None, same_sync=True):
        deps = {}

        def need(d):
            for s, v in d.items():
                if deps.get(s, 0) < v:
                    deps[s] = v

        for o in reads:
            need(o.writes)
        for o in list(writes) + list(pwrites):
            need(o.writes)
            need(o.reads)
        own = "eng_" + eng
        kn = self.known[eng]
        lst = self.lists[eng]
        for s, v in deps.items():
            if s == own and (eng == "pe" or not same_sync):
                continue
            if kn.get(s, 0) < v:
                lst.append(("w", s, v))
                kn[s] = v
        if dma is None:
            sname, inc = own, 1
        else:
            sname, inc = "dma_" + dma, 16
        se = self._sem(sname)
        se[1] += inc
        val = se[1]
        lst.append(("o", fn, sname, inc))
        for o in writes:
            o.writes = {sname: val}
            o.reads = {}
        for o in pwrites:
            if o.writes.get(sname, 0) < val:
                o.writes[sname] = val
        for o in reads:
            if o.reads.get(sname, 0) < val:
                o.reads[sname] = val
        self.nops += 1
        return sname, val

    def final_wait(self, eng, objs):
        deps = {}
        for o in objs:
            for s, v in o.writes.items():
                deps[s] = max(deps.get(s, 0), v)
        for s, v in deps.items():
            self.lists[eng].append(("w", s, v))

    def emit(self):
        nc = self.nc
        sems = self.sems

        def replay(name):
            def f(e):
                for it in self.lists[name]:
                    if it[0] == "w":
                        e.wait_ge(sems[it[1]][0], it[2])
                    else:
                        ins = it[1](e)
                        ins.then_inc(sems[it[2]][0], it[3])
            return f

        with nc.Block() as block:
            block.tensor(replay("pe"))
            block.scalar(replay("act"))
            block.vector(replay("dve"))
            block.gpsimd(replay("pool"))
            block.sync(replay("sp"))
        for g in reversed(self.guards):
            g.__exit__(None, None, None)


def small_layout(depth):
    off = {}
    n = 0

    def add(name, w):
        nonlocal n
        off[name] = (n, w)
        n += w

    add("cT", KC)
    add("normg", depth * 3 * KC)
    add("convw", depth * 4 * CW)
    add("convb", depth * 4)
    add("lng", depth * 4)
    add("lnb", depth * 4)
    add("pscale", depth * 4)
    add("finalg", KC)
    add("tri", 128)
    add("negones", 128)
    add("ones", 128)
    add("pfix", 4 * PH)
    return off, n


def pack_small(depth, c_b, norm_g, b_ada, conv_w, conv_b, conv_ln_g, conv_ln_b, pool_scale, final_g):
    off, n = small_layout(depth)
    sm = np.zeros((128, n), np.float32)

    def put(name, arr):
        o, w = off[name]
        assert arr.shape == (128, w), (name, arr.shape, w)
        sm[:, o:o + w] = arr

    put("cT", c_b.reshape(KC, 128).T)
    put("normg", norm_g.reshape(depth * 3 * KC, 128).T)
    put("convw", conv_w.reshape(depth, CW, 4, 128).transpose(3, 0, 2, 1).reshape(128, depth * 4 * CW))
    put("convb", conv_b.reshape(depth * 4, 128).T)
    put("lng", conv_ln_g.reshape(depth * 4, 128).T)
    put("lnb", conv_ln_b.reshape(depth * 4, 128).T)
    put("pscale", pool_scale.reshape(depth * 4, 128).T)
    put("finalg", final_g.reshape(KC, 128).T)
    kk = np.arange(128)
    put("tri", -(kk[:, None] >= kk[None, :]).astype(np.float32))
    put("negones", -np.ones((128, 128), np.float32))
    put("ones", np.ones((128, 128), np.float32))
    pf = np.ones((4, PH), np.float32)
    for gi, w in enumerate(POOL_WINDOWS):
        for tt in range(PH):
            pf[gi, tt] = w / min(tt + 1, w)
    put("pfix", np.broadcast_to(pf.reshape(1, 4 * PH), (128, 4 * PH)))
    return sm


def build_program(S, depth):
    assert S % T == 0
    NT = S // T
    nc = bass.Bass("TRN2", target_bir_lowering=False)
    off, NSM = small_layout(depth)

    def dram_in(name, shape):
        return nc.dram_tensor(name, list(shape), F32, kind="ExternalInput").ap()

    xT = dram_in("xT", (D, S))
    small = dram_in("small", (128, NSM))
    bada_d = dram_in("bada", (128, depth * 144))
    maskf_d = dram_in("maskf", (128, 896))
    w_ada = dram_in("w_ada", (depth, D, NADA * D))
    ffn_gu = [dram_in("ffn1_w_gu", (depth, D, 2 * DFF)), dram_in("ffn2_w_gu", (depth, D, 2 * DFF))]
    ffn_d = [dram_in("ffn1_w_d", (depth, DFF, D)), dram_in("ffn2_w_d", (depth, DFF, D))]
    w_in = dram_in("w_in", (depth, D, INCOLS))
    w_brc = dram_in("w_br_conv", (depth, DCONV, D))
    w_bra = dram_in("w_br_attn", (depth, DATT, D))
    w_brp = dram_in("w_br_pool", (depth, DPOOL, D))
    w_out = dram_in("w_out", (depth, D, D))
    pool_w = dram_in("pool_wT", (128, depth * 4 * 128))
    yT = nc.dram_tensor("yT", [D, S], F32, kind="ExternalOutput").ap()
    kcache = [nc.dram_tensor("kcache%d" % l, [DATT, S], BF16).ap() for l in range(depth)]
    vcache = [nc.dram_tensor("vcache%d" % l, [8, 128, S // 128, 128], BF16).ap() for l in range(depth)]

    P = Prog(nc)
    guards = []

    def sb(name, shape, dt):
        g = nc.sbuf_tensor(name, list(shape), dt)
        t = g.__enter__()
        guards.append(g)
        return t

    def psum(name):
        g = nc.psum_tensor(name, [128, 512], F32)
        t = g.__enter__()
        guards.append(g)
        return t

    hT = sb("hT", (128, KC, T), F32)
    u = sb("u", (128, KC, T), BF16)
    act = sb("act", (128, HC, T), BF16)
    tmpAf = sb("tmpA", (128, PH + T), F32)
    tmpBf = sb("tmpB", (128, PH + T), F32)
    tmpA = tmpAf[:, 0:T]
    tmpB = tmpBf[:, 0:T]
    rstd = sb("rstd", (128, T), F32)
    spb = [sb("sp%d" % i, (128, T), F32R) for i in range(2)]
    Rb = [sb("R%d" % i, (128, T), F32R) for i in range(2)]
    wbf = [sb("wbf%d" % i, (128, T), BF16) for i in range(2)]
    glu = sb("glu", (128, 4, HALO + T), F32)
    pin = sb("pin", (128, 4, PH + T), F32)
    cacc = sb("cacc", (128, 4, T), F32)
    gluh = sb("gluh", (128, depth, 4, HALO), F32)
    pinh = sb("pinh", (128, depth, 4, PH), F32)
    NRING = 3
    wring = [sb("wring%d" % i, (128, KC, 512), BF16) for i in range(NRING)]
    NSEG = 2
    maskb = sb("maskb", (128, 896), BF16)
    ksA = [sb("ksA%d" % i, (128, 512), BF16) for i in range(NSEG)]
    ksB = [sb("ksB%d" % i, (128, 512), BF16) for i in range(NSEG)]
    vsA = [sb("vsA%d" % i, (128, 4, 128), BF16) for i in range(NSEG)]
    vsB = [sb("vsB%d" % i, (128, 4, 128), BF16) for i in range(NSEG)]
    kst = wbf
    ps2, ps4 = tmpAf[:, :], tmpBf[:, :]
    ebuf = glu[:, 0, :]
    sm = sb("sm", (128, NSM), F32)
    pw = sb("pw", (128, depth * 4 * 128), BF16)
    cact = sb("cact", (128, KC), BF16)
    ada = sb("ada", (128, depth, 144), F32)
    Amod = sb("Amod", (128, depth * 3 + 1, KC), F32)
    Gmod = sb("Gmod", (128, depth * 3, KC), F32)
    zcol = sb("zcol", (128, 1), F32)
    onecol = sb("onecol", (128, 1), F32)
    epsD = sb("epsD", (128, 1), F32)
    eps1 = sb("eps1", (128, 1), F32)
    ones_bf = sb("ones_bf", (128, 128), BF16)
    tri_r = sb("tri_r", (128, 128), F32R)
    neg_r = sb("neg_r", (128, 128), F32R)
    banks = [psum("bank%d" % i) for i in range(8)]

    o_h = [Obj("h%d" % i) for i in range(KC)]
    o_u = [Obj("u%d" % i) for i in range(KC)]
    o_act = [Obj("act%d" % i) for i in range(HC)]
    o_tmpA, o_tmpB, o_rstd = Obj("tmpA"), Obj("tmpB"), Obj("rstd")
    o_sp = [Obj("sp0"), Obj("sp1")]
    o_R = [Obj("R0"), Obj("R1")]
    o_wbf = [Obj("wbf0"), Obj("wbf1")]
    o_glu = [Obj("glu%d" % i) for i in range(4)]
    o_pin = [Obj("pin%d" % i) for i in range(4)]
    o_cacc = [Obj("cacc%d" % i) for i in range(4)]
    o_gluh = [Obj("gluh%d" % l) for l in range(depth)]
    o_pinh = [Obj("pinh%d" % l) for l in range(depth)]
    o_ring = [Obj("ring%d" % i) for i in range(NRING)]
    o_ksA = [Obj("ksA%d" % i) for i in range(NSEG)]
    o_ksB = [Obj("ksB%d" % i) for i in range(NSEG)]
    o_vsA = [Obj("vsA%d" % i) for i in range(NSEG)]
    o_vsB = [Obj("vsB%d" % i) for i in range(NSEG)]
    o_kst = o_wbf
    o_ps2, o_ps4 = o_tmpA, o_tmpB
    o_ebuf = o_glu[0]
    o_sm, o_pw, o_cact, o_ada, o_mod, o_const = Obj("sm"), Obj("pw"), Obj("cact"), Obj("ada"), Obj("mod"), Obj("const")
    o_bank = [Obj("bank%d" % i) for i in range(8)]
    o_kc = [Obj("kc%d" % l) for l in range(depth)]
    o_vc = [Obj("vc%d" % l) for l in range(depth)]
    o_y = Obj("y")

    def smv(name, lo=0, hi=None):
        o, w = off[name]
        hi = w if hi is None else hi
        return sm[:, o + lo:o + hi]

    ring_ctr = [0]
    kst_ctr = [0]
    ost_ctr = [0]
    seg_ctr = [0]

    def load_w(src_ap, nk, ncols):
        i = ring_ctr[0] % NRING
        ring_ctr[0] += 1
        dst = wring[i]
        P.op("pool", lambda e, d=dst, s=src_ap, nk=nk, nco=ncols: e.dma_start(
            out=d[:, 0:nk, 0:nco], in_=s.rearrange("(kc p) n -> p kc n", p=128)),
            writes=[o_ring[i]], dma="ring%d" % i)
        return wring[i], o_ring[i]

    def mm(bank_i, out_ap, lhsT, rhs, start, stop, reads):
        if start:
            P.op("pe", lambda e: e.matmul(out_ap, lhsT=lhsT, rhs=rhs, start=True, stop=stop),
                 reads=reads, writes=[o_bank[bank_i]])
        else:
            P.op("pe", lambda e: e.matmul(out_ap, lhsT=lhsT, rhs=rhs, start=False, stop=stop),
                 reads=reads, pwrites=[o_bank[bank_i]])

    pool_ctr = {}

    def next_bank(pool):
        k = tuple(pool)
        c = pool_ctr.get(k, 0)
        pool_ctr[k] = c + 1
        return pool[c % len(pool)]

    ALLB = list(range(8))

    P.op("sp", lambda e: e.dma_start(out=sm[:], in_=small), writes=[o_sm], dma="sm")
    P.op("pool", lambda e: e.dma_start(out=pw[:], in_=pool_w), writes=[o_pw], dma="pw")
    o_maskb = Obj("maskb")
    P.op("pool", lambda e: e.dma_start(out=maskb[:], in_=maskf_d), writes=[o_maskb], dma="maskb")
    badas = cacc[:].rearrange("p a b -> p (a b)")
    P.op("sp", lambda e: e.dma_start(out=badas[:, 0:depth * 144], in_=bada_d), writes=o_cacc, dma="bada")
    P.op("dve", lambda e: e.memset(zcol[:], 0.0), writes=[o_const])
    P.op("dve", lambda e: e.memset(onecol[:], 1.0), pwrites=[o_const])
    P.op("dve", lambda e: e.memset(epsD[:], float(D * EPS)), pwrites=[o_const])
    P.op("dve", lambda e: e.memset(eps1[:], float(EPS)), pwrites=[o_const])
    P.op("dve", lambda e: e.memset(ones_bf[:], 1.0), pwrites=[o_const])
    for i_ in range(NSEG):
        P.op("pool", lambda e, i_=i_: e.memset(ksA[i_][:], 0.0), writes=[o_ksA[i_]])
        P.op("pool", lambda e, i_=i_: e.memset(ksB[i_][:], 0.0), writes=[o_ksB[i_]])
        P.op("pool", lambda e, i_=i_: e.memset(vsA[i_][:], 0.0), writes=[o_vsA[i_]])
        P.op("pool", lambda e, i_=i_: e.memset(vsB[i_][:], 0.0), writes=[o_vsB[i_]])
    P.op("act", lambda e: e.activation(out=cact[:], in_=smv("cT"), func=AF.Silu), reads=[o_sm], writes=[o_cact])
    P.op("dve", lambda e: e.tensor_copy(out=tri_r[:], in_=smv("tri")), reads=[o_sm], pwrites=[o_const])
    P.op("dve", lambda e: e.tensor_copy(out=neg_r[:], in_=smv("negones")), reads=[o_sm], pwrites=[o_const])
    ADAB = 5

    def ada_block(l, blk):
        wt, ow = load_w(w_ada[l, :, blk * 512:(blk + 1) * 512], KC, 512)
        for m in range(4):
            col = blk * 4 + m
            for kc in range(KC):
                firstw = (kc == 0 and col == 0)
                P.op("pe", lambda e, o_=banks[ADAB][:, col:col + 1], a_=wt[:, kc, m * 128:(m + 1) * 128],
                     b_=cact[:, kc:kc + 1], st=(kc == 0), sp_=(kc == KC - 1):
                     e.matmul(o_, lhsT=a_, rhs=b_, start=st, stop=sp_),
                     reads=[ow, o_cact],
                     writes=([o_bank[ADAB]] if firstw else []),
                     pwrites=([] if firstw else [o_bank[ADAB]]))

    def ada_finish(l, bias_ap, bias_objs):
        P.op("dve", lambda e: e.tensor_tensor(out=ada[:, l, :], in0=banks[ADAB][:, 0:144], in1=bias_ap, op=ALU.add),
             reads=[o_bank[ADAB]] + list(bias_objs), pwrites=[o_ada])
        for s in range(3):
            sc = ada[:, l, (3 * s + 1) * KC:(3 * s + 2) * KC]
            gg = ada[:, l, (3 * s + 2) * KC:(3 * s + 3) * KC]
            ng = smv("normg", (l * 3 + s) * KC, (l * 3 + s + 1) * KC)
            P.op("dve", lambda e, sc=sc, ng=ng, s=s: e.scalar_tensor_tensor(
                out=Amod[:, l * 3 + s, :], in0=sc, scalar=1.0, in1=ng, op0=ALU.add, op1=ALU.mult),
                reads=[o_ada, o_sm], pwrites=[o_mod])
            P.op("dve", lambda e, s=s: e.tensor_scalar(
                out=Amod[:, l * 3 + s, :], in0=Amod[:, l * 3 + s, :], scalar1=float(np.sqrt(D)), scalar2=None,
                op0=ALU.mult), reads=[o_mod], pwrites=[o_mod])
            P.op("dve", lambda e, gg=gg, s=s: e.tensor_scalar(
                out=Gmod[:, l * 3 + s, :], in0=gg, scalar1=(1.0 if s == 1 else 0.5), scalar2=None, op0=ALU.mult),
                reads=[o_ada], pwrites=[o_mod])

    for blk in range(36):
        ada_block(0, blk)
    ada_finish(0, badas[:, 0:144], o_cacc)
    P.op("dve", lambda e: e.tensor_scalar(out=Amod[:, depth * 3, :], in0=smv("finalg"),
                                          scalar1=float(np.sqrt(D)), scalar2=None, op0=ALU.mult),
         reads=[o_sm], pwrites=[o_mod])

    def norm_mod(a_idx, sh_ap_fn, out_fn, out_objs):
        sq = act
        for kc in range(KC):
            if kc % 2 == 0:
                P.op("act", lambda e, kc=kc: e.activation(out=sq[:, kc, :], in_=hT[:, kc, :], func=AF.Square),
                     reads=[o_h[kc]], writes=[o_act[kc]])
            else:
                P.op("dve", lambda e, kc=kc: e.tensor_tensor(out=sq[:, kc, :], in0=hT[:, kc, :], in1=hT[:, kc, :], op=ALU.mult),
                     reads=[o_h[kc]], writes=[o_act[kc]])
        b = next_bank(ALLB)
        for kc in range(KC):
            mm(b, banks[b][:, :], ones_bf[:], sq[:, kc, :], start=(kc == 0), stop=(kc == KC - 1),
               reads=[o_act[kc], o_const])
        P.op("act", lambda e, b=b: e.activation(out=rstd[:], in_=banks[b][:, :], func=AF.Ln, bias=epsD[:, 0:1], scale=1.0),
             reads=[o_bank[b], o_const], writes=[o_rstd])
        P.op("act", lambda e: e.activation(out=rstd[:], in_=rstd[:], func=AF.Exp, scale=-0.5),
             reads=[o_rstd], writes=[o_rstd])
        for kc in range(KC):
            tm, otm = (tmpA, o_tmpA) if kc % 2 == 0 else (tmpB, o_tmpB)
            P.op("dve", lambda e, kc=kc, tm=tm: e.tensor_tensor(out=tm[:], in0=hT[:, kc, :], in1=rstd[:], op=ALU.mult),
                 reads=[o_h[kc], o_rstd], writes=[otm])
            P.op("act", lambda e, kc=kc, tm=tm: e.activation(
                out=out_fn(kc), in_=tm[:], func=AF.Identity, scale=Amod[:, a_idx, kc:kc + 1], bias=sh_ap_fn(kc)),
                reads=[otm, o_mod, o_ada, o_const], writes=[out_objs[kc]])

    def ffn(l, which):
        s = 0 if which == 0 else 2
        wgu = ffn_gu[which]
        wd = ffn_d[which]
        norm_mod(l * 3 + s, lambda kc: ada[:, l, (3 * s) * KC + kc:(3 * s) * KC + kc + 1],
                 lambda kc: u[:, kc, :], o_u)
        nblk = (HC + 3) // 4
        for jb in range(nblk):
            nj = min(4, HC - jb * 4)
            wu, owu = load_w(wgu[l, :, DFF + jb * 512:DFF + jb * 512 + nj * 128], KC, nj * 128)
            wg, owg = load_w(wgu[l, :, jb * 512:jb * 512 + nj * 128], KC, nj * 128)
            bus = []
            for jl in range(nj):
                bu = next_bank(ALLB)
                bus.append(bu)
                for kc in range(KC):
                    mm(bu, banks[bu][:, :], wu[:, kc, jl * 128:(jl + 1) * 128], u[:, kc, :],
                       start=(kc == 0), stop=(kc == KC - 1), reads=[owu, o_u[kc]])
            for _ in range(4 - nj):
                next_bank(ALLB)
            for jl in range(nj):
                j = jb * 4 + jl
                bu = bus[jl]
                bg = next_bank(ALLB)
                for kc in range(KC):
                    mm(bg, banks[bg][:, :], wg[:, kc, jl * 128:(jl + 1) * 128], u[:, kc, :],
                       start=(kc == 0), stop=(kc == KC - 1), reads=[owg, o_u[kc]])
                tm, otm = (tmpA, o_tmpA) if j % 2 == 0 else (tmpB, o_tmpB)
                P.op("act", lambda e, bg=bg, tm=tm: e.activation(out=tm[:], in_=banks[bg][:, :], func=AF.Silu),
                     reads=[o_bank[bg]], writes=[otm])
                P.op("dve", lambda e, bu=bu, tm=tm, j=j: e.tensor_tensor(out=act[:, j, :], in0=banks[bu][:, :],
                                                                         in1=tm[:], op=ALU.mult),
                     reads=[o_bank[bu], otm], writes=[o_act[j]])
            for _ in range(4 - nj):
                next_bank(ALLB)
        parts = [(0, 16), (16, 16), (32, HC - 32)]
        for og in range(4):
            bs = [next_bank(ALLB) for _ in range(4)]
            for (k0, nk) in parts:
                wt, ow = load_w(wd[l, k0 * 128:(k0 + nk) * 128, og * 512:(og + 1) * 512], nk, 512)
                for o in range(4):
                    for kk in range(nk):
                        kc = k0 + kk
                        mm(bs[o], banks[bs[o]][:, :], wt[:, kk, o * 128:(o + 1) * 128], act[:, kc, :],
                           start=(kc == 0), stop=(kc == HC - 1), reads=[ow, o_act[kc]])
            for o in range(4):
                fc = og * 4 + o
                P.op("dve", lambda e, b=bs[o], fc=fc: e.scalar_tensor_tensor(
                    out=hT[:, fc, :], in0=banks[b][:, :], scalar=Gmod[:, l * 3 + s, fc:fc + 1], in1=hT[:, fc, :],
                    op0=ALU.mult, op1=ALU.add), reads=[o_bank[bs[o]], o_mod, o_h[fc]], writes=[o_h[fc]])

    Q0, AO0, MG0, CA0, PA0 = 0, 8, 16, 32, 36

    def proj_fm(l, wblk, owblk, jl, bank):
        for kc in range(KC):
            mm(bank, banks[bank][:, :], wblk[:, kc, jl * 128:(jl + 1) * 128], u[:, kc, :],
               start=(kc == 0), stop=(kc == KC - 1), reads=[owblk, o_u[kc]])

    def mixer(l, t):
        s = 1
        norm_mod(l * 3 + s, lambda kc: ada[:, l, (3 * s) * KC + kc:(3 * s) * KC + kc + 1],
                 lambda kc: u[:, kc, :], o_u)
        wa, owa = load_w(w_in[l, :, 0:512], KC, 512)
        wg, owg = load_w(w_in[l, :, 512:1024], KC, 512)
        for ch in range(4):
            if t == 0:
                P.op("dve", lambda e, ch=ch: e.memset(glu[:, ch, 0:HALO], 0.0), pwrites=[o_glu[ch]])
            else:
                P.op("dve", lambda e, ch=ch: e.tensor_copy(out=glu[:, ch, 0:HALO], in_=gluh[:, l, ch, :]),
                     reads=[o_gluh[l]], pwrites=[o_glu[ch]])
        bas = []
        for ch in range(4):
            ba = next_bank(ALLB)
            bas.append(ba)
            proj_fm(l, wa, owa, ch, ba)
        for ch in range(4):
            ba = bas[ch]
            bg = next_bank(ALLB)
            proj_fm(l, wg, owg, ch, bg)
            tm, otm = (tmpA, o_tmpA) if ch % 2 == 0 else (tmpB, o_tmpB)
            P.op("act", lambda e, bg=bg, tm=tm: e.activation(out=tm[:], in_=banks[bg][:, :], func=AF.Sigmoid),
                 reads=[o_bank[bg]], writes=[otm])
            P.op("dve", lambda e, ba=ba, ch=ch, tm=tm: e.tensor_tensor(out=glu[:, ch, HALO:HALO + T], in0=banks[ba][:, :],
                                                                       in1=tm[:], op=ALU.mult),
                 reads=[o_bank[ba], otm], pwrites=[o_glu[ch]])
        if t + 1 < NT:
            for ch in range(4):
                P.op("dve", lambda e, ch=ch: e.tensor_copy(out=gluh[:, l, ch, :], in_=glu[:, ch, T:T + HALO]),
                     reads=[o_glu[ch]], pwrites=[o_gluh[l]])
        for qb in range(2):
            wq, owq = load_w(w_in[l, :, 1024 + qb * 512:1024 + (qb + 1) * 512], KC, 512)
            for jl in range(4):
                c = qb * 4 + jl
                b = next_bank(ALLB)
                proj_fm(l, wq, owq, jl, b)
                P.op("act", lambda e, b=b, c=c: e.activation(out=act[:, Q0 + c, :], in_=banks[b][:, :],
                                                             func=AF.Copy, scale=0.125),
                     reads=[o_bank[b]], writes=[o_act[Q0 + c]])
        for kb_ in range(2):
            wk, owk = load_w(w_in[l, :, 2048 + kb_ * 512:2048 + (kb_ + 1) * 512], KC, 512)
            for jl in range(4):
                c = kb_ * 4 + jl
                b = next_bank(ALLB)
                proj_fm(l, wk, owk, jl, b)
                i = kst_ctr[0] % 2
                kst_ctr[0] += 1
                P.op("act", lambda e, b=b, i=i: e.activation(out=kst[i][:], in_=banks[b][:, :], func=AF.Copy),
                     reads=[o_bank[b]], writes=[o_kst[i]])
                P.op("sp", lambda e, i=i, c=c: e.dma_start(out=kcache[l][c * 128:(c + 1) * 128, t * T:(t + 1) * T],
                                                           in_=kst[i][:]),
                     reads=[o_kst[i]], pwrites=[o_kc[l]], dma="kst%d" % i)
        for vb in range(2):
            wv, owv = load_w(w_in[l, :, 3072 + vb * 512:3072 + (vb + 1) * 512], KC, 512)
            for sub in range(4):
                b = next_bank(ALLB)
                for kc in range(KC):
                    mm(b, banks[b][:, :], u[:, kc, sub * 128:(sub + 1) * 128], wv[:, kc, :],
                       start=(kc == 0), stop=(kc == KC - 1), reads=[owv, o_u[kc]])
                i = kst_ctr[0] % 2
                kst_ctr[0] += 1
                P.op("act", lambda e, b=b, i=i: e.activation(out=kst[i][:], in_=banks[b][:, :], func=AF.Copy),
                     reads=[o_bank[b]], writes=[o_kst[i]])
                kb = t * 4 + sub
                P.op("sp", lambda e, i=i, vb=vb, kb=kb: e.dma_start(
                    out=vcache[l][vb * 4:(vb + 1) * 4, :, kb, :].rearrange("c p f -> p c f"),
                    in_=kst[i][:].rearrange("p (c f) -> p c f", f=128)),
                    reads=[o_kst[i]], pwrites=[o_vc[l]], dma="kst%d" % i)
        wp, owp = load_w(w_in[l, :, 4096:4608], KC, 512)
        for gi in range(4):
            if t == 0:
                P.op("dve", lambda e, gi=gi: e.memset(pin[:, gi, 0:PH], 0.0), pwrites=[o_pin[gi]])
            else:
                P.op("dve", lambda e, gi=gi: e.tensor_copy(out=pin[:, gi, 0:PH], in_=pinh[:, l, gi, :]),
                     reads=[o_pinh[l]], pwrites=[o_pin[gi]])
            b = next_bank(ALLB)
            proj_fm(l, wp, owp, gi, b)
            P.op("act", lambda e, b=b, gi=gi: e.activation(out=pin[:, gi, PH:PH + T], in_=banks[b][:, :], func=AF.Copy),
                 reads=[o_bank[b]], pwrites=[o_pin[gi]])
            if t + 1 < NT:
                P.op("dve", lambda e, gi=gi: e.tensor_copy(out=pinh[:, l, gi, :], in_=pin[:, gi, T:T + PH]),
                     reads=[o_pin[gi]], pwrites=[o_pinh[l]])

        def cw(ch, j):
            o_ = off["convw"][0] + (l * 4 + ch) * CW + j
            return sm[:, o_:o_ + 1]

        conv_ops = []
        for ch in range(4):
            cb = smv("convb", l * 4 + ch, l * 4 + ch + 1)
            conv_ops.append(lambda ch=ch, cb=cb: P.op("dve", lambda e: e.tensor_scalar(
                out=cacc[:, ch, :], in0=glu[:, ch, 0:T], scalar1=cw(ch, 0), scalar2=cb, op0=ALU.mult, op1=ALU.add),
                reads=[o_glu[ch], o_sm], writes=[o_cacc[ch]]))
        for j in range(1, CW):
            for ch in range(4):
                conv_ops.append(lambda ch=ch, j=j: P.op("dve", lambda e: e.scalar_tensor_tensor(
                    out=cacc[:, ch, :], in0=glu[:, ch, j:j + T], scalar=cw(ch, j), in1=cacc[:, ch, :],
                    op0=ALU.mult, op1=ALU.add), reads=[o_glu[ch], o_sm, o_cacc[ch]], writes=[o_cacc[ch]]))

        def conv_ln():
            b1 = next_bank(ALLB)
            for ch in range(4):
                mm(b1, banks[b1][:, :], smv("ones"), cacc[:, ch, :], start=(ch == 0), stop=(ch == 3),
                   reads=[o_cacc[ch], o_sm])
            b2 = next_bank(ALLB)
            for ch in range(4):
                tm, otm = (tmpA, o_tmpA) if ch % 2 == 0 else (tmpB, o_tmpB)
                P.op("act", lambda e, ch=ch, tm=tm: e.activation(out=tm[:], in_=cacc[:, ch, :], func=AF.Square),
                     reads=[o_cacc[ch]], writes=[otm])
                mm(b2, banks[b2][:, :], smv("ones"), tm[:], start=(ch == 0), stop=(ch == 3), reads=[otm, o_sm])
            P.op("dve", lambda e: e.tensor_scalar(out=ebuf[:, 0:T], in0=banks[b1][:, :], scalar1=1.0 / DCONV, scalar2=None,
                                                  op0=ALU.mult), reads=[o_bank[b1]], writes=[o_ebuf])
            P.op("dve", lambda e: e.tensor_tensor(out=tmpA[:], in0=ebuf[:, 0:T], in1=ebuf[:, 0:T], op=ALU.mult),
                 reads=[o_ebuf], writes=[o_tmpA])
            P.op("dve", lambda e: e.scalar_tensor_tensor(out=rstd[:], in0=banks[b2][:, :], scalar=1.0 / DCONV, in1=tmpA[:],
                                                         op0=ALU.mult, op1=ALU.subtract),
                 reads=[o_bank[b2], o_tmpA], writes=[o_rstd])
            P.op("act", lambda e: e.activation(out=rstd[:], in_=rstd[:], func=AF.Ln, bias=eps1[:, 0:1], scale=1.0),
                 reads=[o_rstd, o_const], writes=[o_rstd])
            P.op("act", lambda e: e.activation(out=rstd[:], in_=rstd[:], func=AF.Exp, scale=-0.5),
                 reads=[o_rstd], writes=[o_rstd])
            for ch in range(4):
                tm, otm = (tmpA, o_tmpA) if ch % 2 == 0 else (tmpB, o_tmpB)
                P.op("dve", lambda e, ch=ch, tm=tm: e.tensor_tensor(out=tm[:], in0=cacc[:, ch, :], in1=ebuf[:, 0:T],
                                                                    op=ALU.subtract),
                     reads=[o_cacc[ch], o_ebuf], writes=[otm])
                P.op("dve", lambda e, tm=tm: e.tensor_tensor(out=tm[:], in0=tm[:], in1=rstd[:], op=ALU.mult),
                     reads=[otm, o_rstd], writes=[otm])
                P.op("act", lambda e, ch=ch, tm=tm: e.activation(
                    out=act[:, CA0 + ch, :], in_=tm[:], func=AF.Silu,
                    scale=smv("lng", l * 4 + ch, l * 4 + ch + 1), bias=smv("lnb", l * 4 + ch, l * 4 + ch + 1)),
                    reads=[otm, o_sm], writes=[o_act[CA0 + ch]])

        W_ = PH + T
        for gi, win in enumerate(POOL_WINDOWS):
            x_ = pin[:, gi, :]
            P.op("dve", lambda e, x_=x_: e.tensor_tensor(out=ps2[:, 1:W_], in0=x_[:, 1:W_], in1=x_[:, 0:W_ - 1], op=ALU.add),
                 reads=[o_pin[gi]], writes=[o_ps2])
            cur, ocur, oth, ooth = ps2, o_ps2, ps4, o_ps4
            sh = 2
            lo = 1
            while sh < win:
                lo2 = lo + sh
                P.op("dve", lambda e, cur=cur, oth=oth, sh=sh, lo2=lo2: e.tensor_tensor(
                    out=oth[:, lo2:W_], in0=cur[:, lo2:W_], in1=cur[:, lo2 - sh:W_ - sh], op=ALU.add),
                    reads=[ocur], writes=[ooth])
                cur, ocur, oth, ooth = oth, ooth, cur, ocur
                lo = lo2
                sh *= 2
            if t == 0:
                P.op("dve", lambda e, cur=cur, gi=gi: e.tensor_tensor(
                    out=cur[:, PH:2 * PH], in0=cur[:, PH:2 * PH], in1=smv("pfix", gi * PH, (gi + 1) * PH), op=ALU.mult),
                    reads=[ocur, o_sm], pwrites=[ocur])
            P.op("dve", lambda e, cur=cur, x_=x_, win=win: e.scalar_tensor_tensor(
                out=wbf[0][:], in0=cur[:, PH:W_], scalar=1.0 / win, in1=x_[:, PH:W_], op0=ALU.mult, op1=ALU.subtract),
                reads=[ocur, o_pin[gi]], writes=[o_wbf[0]])
            b = next_bank(ALLB)
            o_ = (l * 4 + gi) * 128
            mm(b, banks[b][:, :], pw[:, o_:o_ + 128], wbf[0][:], start=True, stop=True, reads=[o_pw, o_wbf[0]])
            P.op("act", lambda e, b=b, gi=gi: e.activation(
                out=act[:, PA0 + gi, :], in_=banks[b][:, :], func=AF.Identity,
                scale=smv("pscale", l * 4 + gi, l * 4 + gi + 1), bias=0.0),
                reads=[o_bank[b], o_sm], writes=[o_act[PA0 + gi]])

        ZB = [0, 1, 2, 3, 4]
        defer_ada = (t == 0 and l + 1 < depth)
        nkb_tot = 4 * (t + 1)
        nseg = t + 1
        items = []
        for c in range(8):
            first = [True, True]
            for sg in reversed(range(nseg)):
                kb_lo = sg * 4
                for hh in range(2):
                    for kb in reversed(range(kb_lo, kb_lo + 4)):
                        items.append(dict(c=c, sg=sg, hh=hh, kb=kb, kl=kb - kb_lo, d=kb - 4 * t,
                                          first=first[hh], last=(kb == 0),
                                          newseg=(hh == 0 and kb == kb_lo + 3),
                                          endc=(sg == 0 and hh == 1 and kb == 0)))
                        first[hh] = False
        n_it = len(items)

        def load_seg(it):
            i = seg_ctr[0] % NSEG
            seg_ctr[0] += 1
            c, kb_lo = it["c"], it["sg"] * 4
            P.op("sp", lambda e: e.dma_start(out=ksA[i][0:64, :], in_=kcache[l][c * 128:c * 128 + 64, kb_lo * 128:(kb_lo + 4) * 128]),
                 reads=[o_kc[l]], pwrites=[o_ksA[i]], dma="ksA%d" % i)
            P.op("sp", lambda e: e.dma_start(out=ksB[i][64:128, :], in_=kcache[l][c * 128 + 64:c * 128 + 128, kb_lo * 128:(kb_lo + 4) * 128]),
                 reads=[o_kc[l]], pwrites=[o_ksB[i]], dma="ksB%d" % i)
            P.op("sp", lambda e: e.dma_start(out=vsA[i][:, :, 0:64], in_=vcache[l][c, :, kb_lo:kb_lo + 4, 0:64]),
                 reads=[o_vc[l]], pwrites=[o_vsA[i]], dma="vsA%d" % i)
            P.op("sp", lambda e: e.dma_start(out=vsB[i][:, :, 64:128], in_=vcache[l][c, :, kb_lo:kb_lo + 4, 64:128]),
                 reads=[o_vc[l]], pwrites=[o_vsB[i]], dma="vsB%d" % i)
            return i

        def stA(it, idx):
            if it["newseg"]:
                it["seg"] = load_seg(it)
            else:
                it["seg"] = items[idx - 1]["seg"]
            i, hh, kl, c = it["seg"], it["hh"], it["kl"], it["c"]
            zb = next_bank(ZB)
            it["zb"] = zb
            kt, okt = (ksA[i], o_ksA[i]) if hh == 0 else (ksB[i], o_ksB[i])
            mm(zb, banks[zb][:, :], kt[:, kl * 128:(kl + 1) * 128], act[:, Q0 + c, :], start=True, stop=False,
               reads=[okt, o_act[Q0 + c]])

        def stB(it, idx):
            zb, d = it["zb"], it["d"]
            si = idx % 2
            it["si"] = si
            sp_, osp = spb[si], o_sp[si]
            eb, oeb = (tmpA, o_tmpA) if si == 0 else (tmpB, o_tmpB)
            P.op("act", lambda e: e.activation(out=eb[:], in_=banks[zb][:, :], func=AF.Exp),
                 reads=[o_bank[zb]], writes=[oeb])

        def stB2(it, idx):
            zb, d = it["zb"], it["d"]
            si = it["si"]
            sp_, osp = spb[si], o_sp[si]
            eb, oeb = (tmpA, o_tmpA) if si == 0 else (tmpB, o_tmpB)
            P.op("act", lambda e: e.activation(out=sp_[:], in_=eb[:], func=AF.Ln, bias=1.0, scale=1.0),
                 reads=[oeb], writes=[osp])
            if d >= 0:
                mo = 384 - 128 * d
                P.op("dve", lambda e: e.tensor_tensor(out=sp_[:], in0=sp_[:].bitcast(F32), in1=maskb[:, mo:mo + T], op=ALU.mult),
                     reads=[osp, o_maskb], writes=[osp])

        def stC(it, idx):
            zb, hh, si = it["zb"], it["hh"], it["si"]
            sp_, osp = spb[si], o_sp[si]
            mm(zb, banks[zb][:, :], tri_r[:], sp_[:], start=False, stop=it["first"], reads=[osp, o_const])
            if not it["first"]:
                mm(zb, banks[zb][:, :], neg_r[:], Rb[hh][:], start=False, stop=True, reads=[o_R[hh], o_const])
            if not it["last"]:
                if it["first"]:
                    P.op("dve", lambda e: e.tensor_copy(out=Rb[hh][:], in_=sp_[:].bitcast(F32)),
                         reads=[osp], writes=[o_R[hh]])
                else:
                    P.op("dve", lambda e: e.tensor_tensor(out=Rb[hh][:], in0=Rb[hh][:].bitcast(F32), in1=sp_[:].bitcast(F32),
                                                          op=ALU.add), reads=[osp, o_R[hh]], writes=[o_R[hh]])

        def stD(it, idx):
            zb, d, si = it["zb"], it["d"], it["si"]
            wb_, owb = wbf[si], o_wbf[si]
            P.op("act", lambda e: e.activation(out=wb_[:], in_=banks[zb][:, :], func=AF.Exp),
                 reads=[o_bank[zb]], writes=[owb])
            if d >= 0:
                mo = 384 - 128 * d
                P.op("dve", lambda e: e.tensor_tensor(out=wb_[:], in0=wb_[:], in1=maskb[:, mo:mo + T], op=ALU.mult),
                     reads=[owb, o_maskb], writes=[owb])

        def stE(it, idx):
            i, hh, kl, c, si = it["seg"], it["hh"], it["kl"], it["c"], it["si"]
            avb = 6 + (c % 2)
            wb_, owb = wbf[si], o_wbf[si]
            vt, ovt = (vsA[i], o_vsA[i]) if hh == 0 else (vsB[i], o_vsB[i])
            mm(avb, banks[avb][:, :], vt[:, kl, :], wb_[:], start=(it["first"] and hh == 0), stop=(it["last"] and hh == 1),
               reads=[ovt, owb])
            if it["endc"]:
                P.op("dve", lambda e: e.tensor_copy(out=act[:, AO0 + c, :], in_=banks[avb][:, :]),
                     reads=[o_bank[avb]], writes=[o_act[AO0 + c]])

        per_item = 2 if t == 0 else 1
        for s_ in range(n_it + 2):
            if s_ < n_it:
                stA(items[s_], s_)
                stB(items[s_], s_)
                stB2(items[s_], s_)
                for _ in range(per_item):
                    if conv_ops:
                        conv_ops.pop(0)()
                if defer_ada and s_ < 36:
                    ada_block(l + 1, s_)
            if 0 <= s_ - 1 < n_it:
                stC(items[s_ - 1], s_ - 1)
                stD(items[s_ - 1], s_ - 1)
            if 0 <= s_ - 2 < n_it:
                stE(items[s_ - 2], s_ - 2)
        while conv_ops:
            conv_ops.pop(0)()
        if defer_ada:
            P.op("sp", lambda e: e.dma_start(out=rstd[:, 0:144], in_=bada_d[:, (l + 1) * 144:(l + 2) * 144]),
                 writes=[o_rstd], dma="bada2")
            ada_finish(l + 1, rstd[:, 0:144], [o_rstd])
        conv_ln()

        br_w = [(w_brc, 4, CA0), (w_bra, 8, AO0), (w_brp, 4, PA0)]
        for fg in range(4):
            for r in range(3):
                wgt, owgt = load_w(w_in[l, :, 4608 + r * D + fg * 512:4608 + r * D + (fg + 1) * 512], KC, 512)
                wsrc, nk, a0 = br_w[r]
                wbr, owbr = load_w(wsrc[l, :, fg * 512:(fg + 1) * 512], nk, 512)
                bgts = []
                for jl in range(4):
                    bgt = next_bank(ALLB)
                    bgts.append(bgt)
                    proj_fm(l, wgt, owgt, jl, bgt)
                for jl in range(4):
                    fc = fg * 4 + jl
                    bgt = bgts[jl]
                    by = next_bank(ALLB)
                    for kk in range(nk):
                        mm(by, banks[by][:, :], wbr[:, kk, jl * 128:(jl + 1) * 128], act[:, a0 + kk, :],
                           start=(kk == 0), stop=(kk == nk - 1), reads=[owbr, o_act[a0 + kk]])
                    P.op("act", lambda e, bgt=bgt: e.activation(out=tmpA[:], in_=banks[bgt][:, :], func=AF.Sigmoid),
                         reads=[o_bank[bgt]], writes=[o_tmpA])
                    if r == 0:
                        P.op("dve", lambda e, by=by, jl=jl: e.tensor_tensor(out=cacc[:, jl, :], in0=banks[by][:, :],
                                                                            in1=tmpA[:], op=ALU.mult),
                             reads=[o_bank[by], o_tmpA], writes=[o_cacc[jl]])
                    else:
                        P.op("dve", lambda e, by=by: e.tensor_tensor(out=tmpB[:], in0=banks[by][:, :], in1=tmpA[:],
                                                                     op=ALU.mult),
                             reads=[o_bank[by], o_tmpA], writes=[o_tmpB])
                        if r == 1:
                            P.op("dve", lambda e, jl=jl: e.tensor_tensor(out=cacc[:, jl, :], in0=cacc[:, jl, :],
                                                                         in1=tmpB[:], op=ALU.add),
                                 reads=[o_cacc[jl], o_tmpB], writes=[o_cacc[jl]])
                        else:
                            P.op("dve", lambda e, jl=jl, fc=fc: e.tensor_tensor(out=act[:, MG0 + fc, :], in0=cacc[:, jl, :],
                                                                                in1=tmpB[:], op=ALU.add),
                                 reads=[o_cacc[jl], o_tmpB], writes=[o_act[MG0 + fc]])
        for og in range(4):
            wo, owo = load_w(w_out[l, :, og * 512:(og + 1) * 512], KC, 512)
            for o in range(4):
                fc = og * 4 + o
                b = next_bank(ALLB)
                for kc in range(KC):
                    mm(b, banks[b][:, :], wo[:, kc, o * 128:(o + 1) * 128], act[:, MG0 + kc, :],
                       start=(kc == 0), stop=(kc == KC - 1), reads=[owo, o_act[MG0 + kc]])
                P.op("dve", lambda e, b=b, fc=fc: e.scalar_tensor_tensor(
                    out=hT[:, fc, :], in0=banks[b][:, :], scalar=Gmod[:, l * 3 + 1, fc:fc + 1], in1=hT[:, fc, :],
                    op0=ALU.mult, op1=ALU.add), reads=[o_bank[b], o_mod, o_h[fc]], writes=[o_h[fc]])

    for t in range(NT):
        for q4 in range(4):
            P.op("sp", lambda e, q4=q4, t=t: e.dma_start(
                out=hT[:, q4 * 4:(q4 + 1) * 4, :],
                in_=xT[q4 * 512:(q4 + 1) * 512, t * T:(t + 1) * T].rearrange("(kc p) s -> p kc s", p=128)),
                writes=o_h[q4 * 4:(q4 + 1) * 4], dma="xin%d" % q4)
        for l in range(depth):
            ffn(l, 0)
            mixer(l, t)
            ffn(l, 1)

        def out_fn(kc):
            return ost[kc % 2][:]

        sq = act
        for kc in range(KC):
            if kc % 2 == 0:
                P.op("act", lambda e, kc=kc: e.activation(out=sq[:, kc, :], in_=hT[:, kc, :], func=AF.Square),
                     reads=[o_h[kc]], writes=[o_act[kc]])
            else:
                P.op("dve", lambda e, kc=kc: e.tensor_tensor(out=sq[:, kc, :], in0=hT[:, kc, :], in1=hT[:, kc, :], op=ALU.mult),
                     reads=[o_h[kc]], writes=[o_act[kc]])
        b = next_bank(ALLB)
        for kc in range(KC):
            mm(b, banks[b][:, :], ones_bf[:], sq[:, kc, :], start=(kc == 0), stop=(kc == KC - 1),
               reads=[o_act[kc], o_const])
        P.op("act", lambda e, b=b: e.activation(out=rstd[:], in_=banks[b][:, :], func=AF.Ln, bias=epsD[:, 0:1], scale=1.0),
             reads=[o_bank[b], o_const], writes=[o_rstd])
        P.op("act", lambda e: e.activation(out=rstd[:], in_=rstd[:], func=AF.Exp, scale=-0.5),
             reads=[o_rstd], writes=[o_rstd])
        for kc in range(KC):
            tm, otm = (tmpA, o_tmpA) if kc % 2 == 0 else (tmpB, o_tmpB)
            P.op("dve", lambda e, kc=kc, tm=tm: e.tensor_tensor(out=tm[:], in0=hT[:, kc, :], in1=rstd[:], op=ALU.mult),
                 reads=[o_h[kc], o_rstd], writes=[otm])
            P.op("act", lambda e, kc=kc, tm=tm: e.activation(
                out=tm[:], in_=tm[:], func=AF.Identity, scale=Amod[:, depth * 3, kc:kc + 1], bias=0.0),
                reads=[otm, o_mod, o_const], writes=[otm])
            P.op("sp", lambda e, kc=kc, tm=tm, t=t: e.dma_start(out=yT[kc * 128:(kc + 1) * 128, t * T:(t + 1) * T],
                                                               in_=tm[:]),
                 reads=[otm], pwrites=[o_y], dma="ost%d" % (kc % 2))
    P.final_wait("sp", [o_y])
    P.emit()
    for g in reversed(guards):
        g.__exit__(None, None, None)
    return nc, P


_CACHE = {}


def make_in_maps(inputs, S, depth, batch_ids):
    f = lambda k: np.ascontiguousarray(np.asarray(inputs[k], dtype=np.float32))
    shared = {
        "w_ada": f("w_ada"), "ffn1_w_gu": f("ffn1_w_gu"), "ffn2_w_gu": f("ffn2_w_gu"),
        "ffn1_w_d": f("ffn1_w_d"), "ffn2_w_d": f("ffn2_w_d"), "w_in": f("w_in"),
        "w_br_conv": f("w_br_conv"), "w_br_attn": f("w_br_attn"), "w_br_pool": f("w_br_pool"), "w_out": f("w_out"),
    }
    pw = f("pool_w")
    shared["pool_wT"] = np.ascontiguousarray(pw.transpose(2, 0, 1, 3).reshape(128, depth * 4 * 128))
    x = f("x")
    c = f("c")
    maps = []
    for b in batch_ids:
        m = dict(shared)
        m["xT"] = np.ascontiguousarray(x[b].T)
        m["bada"] = np.ascontiguousarray(f("b_ada").reshape(depth * 144, 128).T)
        kk_ = np.arange(128)
        jj_ = np.arange(896)
        m["maskf"] = (kk_[:, None] < (jj_[None, :] - 384)).astype(np.float32)
        m["small"] = pack_small(depth, c[b], f("norm_g"), f("b_ada"), f("conv_w"), f("conv_b"), f("conv_ln_g"),
                                f("conv_ln_b"), f("pool_scale"), f("final_g"))
        maps.append(m)
    return maps


def run(inputs, S, depth, trace=False):
    B = np.asarray(inputs["x"]).shape[0]
    key = (S, depth)
    if key not in _CACHE:
        _CACHE[key] = build_program(S, depth)[0]
    nc = _CACHE[key]
    maps = make_in_maps(inputs, S, depth, list(range(B)))
    res = run_bass_kernel_spmd(nc, maps, core_ids=list(range(B)), trace=trace)
    out = np.stack([np.ascontiguousarray(r["yT"].T) for r in res.results], axis=0).astype(np.float32)
    return out, res


def kernel(**inputs):
    out, _ = run(inputs, 4096, 4)
    return out
```

```python
import numpy as np
import concourse.bass as bass
import concourse.mybir as mybir
from concourse.bass_utils import run_bass_kernel_spmd

F32 = mybir.dt.float32
F32R = mybir.dt.float32r
BF16 = mybir.dt.bfloat16
AF = mybir.ActivationFunctionType
ALU = mybir.AluOpType

D = 2048
KC = 16
T = 512
DFF = 5504
HC = 43
DCONV = 512
CW = 31
NH = 16
DATT = 1024
DPOOL = 512
INCOLS = 10752
NADA = 9
EPS = 1e-6
POOL_WINDOWS = (2, 4, 8, 16)
HALO = 30
PH = 16


class Obj:
    __slots__ = ("name", "writes", "reads")

    def __init__(self, name):
        self.name = name
        self.writes = {}
        self.reads = {}


class Prog:
    ENGS = ("pe", "act", "dve", "pool", "sp")

    def __init__(self, nc):
        self.nc = nc
        self.sems = {}
        self.lists = {e: [] for e in self.ENGS}
        self.known = {e: {} for e in self.ENGS}
        self.guards = []
        for e in self.ENGS:
            self._sem("eng_" + e)
        self.nops = 0

    def _sem(self, name):
        if name not in self.sems:
            g = self.nc.semaphore(name)
            h = g.__enter__()
            self.guards.append(g)
            self.sems[name] = [h, 0]
        return self.sems[name]

    def op(self, eng, fn, reads=(), writes=(), pwrites=(), dma=None, same_sync=True):
        deps = {}

        def need(d):
            for s, v in d.items():
                if deps.get(s, 0) < v:
                    deps[s] = v

        for o in reads:
            need(o.writes)
        for o in list(writes) + list(pwrites):
            need(o.writes)
            need(o.reads)
        own = "eng_" + eng
        kn = self.known[eng]
        lst = self.lists[eng]
        for s, v in deps.items():
            if s == own and (eng == "pe" or not same_sync):
                continue
            if kn.get(s, 0) < v:
                lst.append(("w", s, v))
                kn[s] = v
        if dma is None:
            sname, inc = own, 1
        else:
            sname, inc = "dma_" + dma, 16
        se = self._sem(sname)
        se[1] += inc
        val = se[1]
        lst.append(("o", fn, sname, inc))
        for o in writes:
            o.writes = {sname: val}
            o.reads = {}
        for o in pwrites:
            if o.writes.get(sname, 0) < val:
                o.writes[sname] = val
        for o in reads:
            if o.reads.get(sname, 0) < val:
                o.reads[sname] = val
        self.nops += 1
        return sname, val

    def final_wait(self, eng, objs):
        deps = {}
        for o in objs:
            for s, v in o.writes.items():
                deps[s] = max(deps.get(s, 0), v)
        for s, v in deps.items():
            self.lists[eng].append(("w", s, v))

    def emit(self):
        nc = self.nc
        sems = self.sems

        def replay(name):
            def f(e):
                for it in self.lists[name]:
                    if it[0] == "w":
                        e.wait_ge(sems[it[1]][0], it[2])
                    else:
                        ins = it[1](e)
                        ins.then_inc(sems[it[2]][0], it[3])
            return f

        with nc.Block() as block:
            block.tensor(replay("pe"))
            block.scalar(replay("act"))
            block.vector(replay("dve"))
            block.gpsimd(replay("pool"))
            block.sync(replay("sp"))
        for g in reversed(self.guards):
            g.__exit__(None, None, None)


def small_layout(depth):
    off = {}
    n = 0

    def add(name, w):
        nonlocal n
        off[name] = (n, w)
        n += w

    add("cT", KC)
    add("normg", depth * 3 * KC)
    add("convw", depth * 4 * CW)
    add("convb", depth * 4)
    add("lng", depth * 4)
    add("lnb", depth * 4)
    add("pscale", depth * 4)
    add("finalg", KC)
    add("tri", 128)
    add("negones", 128)
    add("ones", 128)
    add("pfix", 4 * PH)
    return off, n


def pack_small(depth, c_b, norm_g, b_ada, conv_w, conv_b, conv_ln_g, conv_ln_b, pool_scale, final_g):
    off, n = small_layout(depth)
    sm = np.zeros((128, n), np.float32)

    def put(name, arr):
        o, w = off[name]
        assert arr.shape == (128, w), (name, arr.shape, w)
        sm[:, o:o + w] = arr

    put("cT", c_b.reshape(KC, 128).T)
    put("normg", norm_g.reshape(depth * 3 * KC, 128).T)
    put("convw", conv_w.reshape(depth, CW, 4, 128).transpose(3, 0, 2, 1).reshape(128, depth * 4 * CW))
    put("convb", conv_b.reshape(depth * 4, 128).T)
    put("lng", conv_ln_g.reshape(depth * 4, 128).T)
    put("lnb", conv_ln_b.reshape(depth * 4, 128).T)
    put("pscale", pool_scale.reshape(depth * 4, 128).T)
    put("finalg", final_g.reshape(KC, 128).T)
    kk = np.arange(128)
    put("tri", -(kk[:, None] >= kk[None, :]).astype(np.float32))
    put("negones", -np.ones((128, 128), np.float32))
    put("ones", np.ones((128, 128), np.float32))
    pf = np.ones((4, PH), np.float32)
    for gi, w in enumerate(POOL_WINDOWS):
        for tt in range(PH):
            pf[gi, tt] = w / min(tt + 1, w)
    put("pfix", np.broadcast_to(pf.reshape(1, 4 * PH), (128, 4 * PH)))
    return sm


def build_program(S, depth):
    assert S % T == 0
    NT = S // T
    nc = bass.Bass("TRN2", target_bir_lowering=False)
    off, NSM = small_layout(depth)

    def dram_in(name, shape):
        return nc.dram_tensor(name, list(shape), F32, kind="ExternalInput").ap()

    xT = dram_in("xT", (D, S))
    small = dram_in("small", (128, NSM))
    bada_d = dram_in("bada", (128, depth * 144))
    maskf_d = dram_in("maskf", (128, 896))
    w_ada = dram_in("w_ada", (depth, D, NADA * D))
    ffn_gu = [dram_in("ffn1_w_gu", (depth, D, 2 * DFF)), dram_in("ffn2_w_gu", (depth, D, 2 * DFF))]
    ffn_d = [dram_in("ffn1_w_d", (depth, DFF, D)), dram_in("ffn2_w_d", (depth, DFF, D))]
    w_in = dram_in("w_in", (depth, D, INCOLS))
    w_brc = dram_in("w_br_conv", (depth, DCONV, D))
    w_bra = dram_in("w_br_attn", (depth, DATT, D))
    w_brp = dram_in("w_br_pool", (depth, DPOOL, D))
    w_out = dram_in("w_out", (depth, D, D))
    pool_w = dram_in("pool_wT", (128, depth * 4 * 128))
    yT = nc.dram_tensor("yT", [D, S], F32, kind="ExternalOutput").ap()
    kcache = [nc.dram_tensor("kcache%d" % l, [DATT, S], BF16).ap() for l in range(depth)]
    vcache = [nc.dram_tensor("vcache%d" % l, [8, 128, S // 128, 128], BF16).ap() for l in range(depth)]

    P = Prog(nc)
    guards = []

    def sb(name, shape, dt):
        g = nc.sbuf_tensor(name, list(shape), dt)
        t = g.__enter__()
        guards.append(g)
        return t

    def psum(name):
        g = nc.psum_tensor(name, [128, 512], F32)
        t = g.__enter__()
        guards.append(g)
        return t

    hT = sb("hT", (128, KC, T), F32)
    u = sb("u", (128, KC, T), BF16)
    act = sb("act", (128, HC, T), BF16)
    tmpAf = sb("tmpA", (128, PH + T), F32)
    tmpBf = sb("tmpB", (128, PH + T), F32)
    tmpA = tmpAf[:, 0:T]
    tmpB = tmpBf[:, 0:T]
    rstd = sb("rstd", (128, T), F32)
    spb = [sb("sp%d" % i, (128, T), F32R) for i in range(2)]
    Rb = [sb("R%d" % i, (128, T), F32R) for i in range(2)]
    wbf = [sb("wbf%d" % i, (128, T), BF16) for i in range(2)]
    glu = sb("glu", (128, 4, HALO + T), F32)
    pin = sb("pin", (128, 4, PH + T), F32)
    cacc = sb("cacc", (128, 4, T), F32)
    gluh = sb("gluh", (128, depth, 4, HALO), F32)
    pinh = sb("pinh", (128, depth, 4, PH), F32)
    NRING = 3
    wring = [sb("wring%d" % i, (128, KC, 512), BF16) for i in range(NRING)]
    NSEG = 2
    maskb = sb("maskb", (128, 896), BF16)
    ksA = [sb("ksA%d" % i, (128, 512), BF16) for i in range(NSEG)]
    ksB = [sb("ksB%d" % i, (128, 512), BF16) for i in range(NSEG)]
    vsA = [sb("vsA%d" % i, (128, 4, 128), BF16) for i in range(NSEG)]
    vsB = [sb("vsB%d" % i, (128, 4, 128), BF16) for i in range(NSEG)]
    kst = wbf
    ps2, ps4 = tmpAf[:, :], tmpBf[:, :]
    ebuf = glu[:, 0, :]
    sm = sb("sm", (128, NSM), F32)
    pw = sb("pw", (128, depth * 4 * 128), BF16)
    cact = sb("cact", (128, KC), BF16)
    ada = sb("ada", (128, depth, 144), F32)
    Amod = sb("Amod", (128, depth * 3 + 1, KC), F32)
    Gmod = sb("Gmod", (128, depth * 3, KC), F32)
    zcol = sb("zcol", (128, 1), F32)
    onecol = sb("onecol", (128, 1), F32)
    epsD = sb("epsD", (128, 1), F32)
    eps1 = sb("eps1", (128, 1), F32)
    ones_bf = sb("ones_bf", (128, 128), BF16)
    tri_r = sb("tri_r", (128, 128), F32R)
    neg_r = sb("neg_r", (128, 128), F32R)
    banks = [psum("bank%d" % i) for i in range(8)]

    o_h = [Obj("h%d" % i) for i in range(KC)]
    o_u = [Obj("u%d" % i) for i in range(KC)]
    o_act = [Obj("act%d" % i) for i in range(HC)]
    o_tmpA, o_tmpB, o_rstd = Obj("tmpA"), Obj("tmpB"), Obj("rstd")
    o_sp = [Obj("sp0"), Obj("sp1")]
    o_R = [Obj("R0"), Obj("R1")]
    o_wbf = [Obj("wbf0"), Obj("wbf1")]
    o_glu = [Obj("glu%d" % i) for i in range(4)]
    o_pin = [Obj("pin%d" % i) for i in range(4)]
    o_cacc = [Obj("cacc%d" % i) for i in range(4)]
    o_gluh = [Obj("gluh%d" % l) for l in range(depth)]
    o_pinh = [Obj("pinh%d" % l) for l in range(depth)]
    o_ring = [Obj("ring%d" % i) for i in range(NRING)]
    o_ksA = [Obj("ksA%d" % i) for i in range(NSEG)]
    o_ksB = [Obj("ksB%d" % i) for i in range(NSEG)]
    o_vsA = [Obj("vsA%d" % i) for i in range(NSEG)]
    o_vsB = [Obj("vsB%d" % i) for i in range(NSEG)]
    o_kst = o_wbf
    o_ps2, o_ps4 = o_tmpA, o_tmpB
    o_ebuf = o_glu[0]
    o_sm, o_pw, o_cact, o_ada, o_mod, o_const = Obj("sm"), Obj("pw"), Obj("cact"), Obj("ada"), Obj("mod"), Obj("const")
    o_bank = [Obj("bank%d" % i) for i in range(8)]
    o_kc = [Obj("kc%d" % l) for l in range(depth)]
    o_vc = [Obj("vc%d" % l) for l in range(depth)]
    o_y = Obj("y")

    def smv(name, lo=0, hi=None):
        o, w = off[name]
        hi = w if hi is None else hi
        return sm[:, o + lo:o + hi]

    ring_ctr = [0]
    kst_ctr = [0]
    ost_ctr = [0]
    seg_ctr = [0]

    def load_w(src_ap, nk, ncols):
        i = ring_ctr[0] % NRING
        ring_ctr[0] += 1
        dst = wring[i]
        P.op("pool", lambda e, d=dst, s=src_ap, nk=nk, nco=ncols: e.dma_start(
            out=d[:, 0:nk, 0:nco], in_=s.rearrange("(kc p) n -> p kc n", p=128)),
            writes=[o_ring[i]], dma="ring%d" % i)
        return wring[i], o_ring[i]

    def mm(bank_i, out_ap, lhsT, rhs, start, stop, reads):
        if start:
            P.op("pe", lambda e: e.matmul(out_ap, lhsT=lhsT, rhs=rhs, start=True, stop=stop),
                 reads=reads, writes=[o_bank[bank_i]])
        else:
            P.op("pe", lambda e: e.matmul(out_ap, lhsT=lhsT, rhs=rhs, start=False, stop=stop),
                 reads=reads, pwrites=[o_bank[bank_i]])

    pool_ctr = {}

    def next_bank(pool):
        k = tuple(pool)
        c = pool_ctr.get(k, 0)
        pool_ctr[k] = c + 1
        return pool[c % len(pool)]

    ALLB = list(range(8))

    P.op("sp", lambda e: e.dma_start(out=sm[:], in_=small), writes=[o_sm], dma="sm")
    P.op("pool", lambda e: e.dma_start(out=pw[:], in_=pool_w), writes=[o_pw], dma="pw")
    o_maskb = Obj("maskb")
    P.op("pool", lambda e: e.dma_start(out=maskb[:], in_=maskf_d), writes=[o_maskb], dma="maskb")
    badas = cacc[:].rearrange("p a b -> p (a b)")
    P.op("sp", lambda e: e.dma_start(out=badas[:, 0:depth * 144], in_=bada_d), writes=o_cacc, dma="bada")
    P.op("dve", lambda e: e.memset(zcol[:], 0.0), writes=[o_const])
    P.op("dve", lambda e: e.memset(onecol[:], 1.0), pwrites=[o_const])
    P.op("dve", lambda e: e.memset(epsD[:], float(D * EPS)), pwrites=[o_const])
    P.op("dve", lambda e: e.memset(eps1[:], float(EPS)), pwrites=[o_const])
    P.op("dve", lambda e: e.memset(ones_bf[:], 1.0), pwrites=[o_const])
    for i_ in range(NSEG):
        P.op("pool", lambda e, i_=i_: e.memset(ksA[i_][:], 0.0), writes=[o_ksA[i_]])
        P.op("pool", lambda e, i_=i_: e.memset(ksB[i_][:], 0.0), writes=[o_ksB[i_]])
        P.op("pool", lambda e, i_=i_: e.memset(vsA[i_][:], 0.0), writes=[o_vsA[i_]])
        P.op("pool", lambda e, i_=i_: e.memset(vsB[i_][:], 0.0), writes=[o_vsB[i_]])
    P.op("act", lambda e: e.activation(out=cact[:], in_=smv("cT"), func=AF.Silu), reads=[o_sm], writes=[o_cact])
    P.op("dve", lambda e: e.tensor_copy(out=tri_r[:], in_=smv("tri")), reads=[o_sm], pwrites=[o_const])
    P.op("dve", lambda e: e.tensor_copy(out=neg_r[:], in_=smv("negones")), reads=[o_sm], pwrites=[o_const])
    ADAB = 5

    def ada_block(l, blk):
        wt, ow = load_w(w_ada[l, :, blk * 512:(blk + 1) * 512], KC, 512)
        for m in range(4):
            col = blk * 4 + m
            for kc in range(KC):
                firstw = (kc == 0 and col == 0)
                P.op("pe", lambda e, o_=banks[ADAB][:, col:col + 1], a_=wt[:, kc, m * 128:(m + 1) * 128],
                     b_=cact[:, kc:kc + 1], st=(kc == 0), sp_=(kc == KC - 1):
                     e.matmul(o_, lhsT=a_, rhs=b_, start=st, stop=sp_),
                     reads=[ow, o_cact],
                     writes=([o_bank[ADAB]] if firstw else []),
                     pwrites=([] if firstw else [o_bank[ADAB]]))

    def ada_finish(l, bias_ap, bias_objs):
        P.op("dve", lambda e: e.tensor_tensor(out=ada[:, l, :], in0=banks[ADAB][:, 0:144], in1=bias_ap, op=ALU.add),
             reads=[o_bank[ADAB]] + list(bias_objs), pwrites=[o_ada])
        for s in range(3):
            sc = ada[:, l, (3 * s + 1) * KC:(3 * s + 2) * KC]
            gg = ada[:, l, (3 * s + 2) * KC:(3 * s + 3) * KC]
            ng = smv("normg", (l * 3 + s) * KC, (l * 3 + s + 1) * KC)
            P.op("dve", lambda e, sc=sc, ng=ng, s=s: e.scalar_tensor_tensor(
                out=Amod[:, l * 3 + s, :], in0=sc, scalar=1.0, in1=ng, op0=ALU.add, op1=ALU.mult),
                reads=[o_ada, o_sm], pwrites=[o_mod])
            P.op("dve", lambda e, s=s: e.tensor_scalar(
                out=Amod[:, l * 3 + s, :], in0=Amod[:, l * 3 + s, :], scalar1=float(np.sqrt(D)), scalar2=None,
                op0=ALU.mult), reads=[o_mod], pwrites=[o_mod])
            P.op("dve", lambda e, gg=gg, s=s: e.tensor_scalar(
                out=Gmod[:, l * 3 + s, :], in0=gg, scalar1=(1.0 if s == 1 else 0.5), scalar2=None, op0=ALU.mult),
                reads=[o_ada], pwrites=[o_mod])

    for blk in range(36):
        ada_block(0, blk)
    ada_finish(0, badas[:, 0:144], o_cacc)
    P.op("dve", lambda e: e.tensor_scalar(out=Amod[:, depth * 3, :], in0=smv("finalg"),
                                          scalar1=float(np.sqrt(D)), scalar2=None, op0=ALU.mult),
         reads=[o_sm], pwrites=[o_mod])

    def norm_mod(a_idx, sh_ap_fn, out_fn, out_objs):
        sq = act
        for kc in range(KC):
            if kc % 2 == 0:
                P.op("act", lambda e, kc=kc: e.activation(out=sq[:, kc, :], in_=hT[:, kc, :], func=AF.Square),
                     reads=[o_h[kc]], writes=[o_act[kc]])
            else:
                P.op("dve", lambda e, kc=kc: e.tensor_tensor(out=sq[:, kc, :], in0=hT[:, kc, :], in1=hT[:, kc, :], op=ALU.mult),
                     reads=[o_h[kc]], writes=[o_act[kc]])
        b = next_bank(ALLB)
        for kc in range(KC):
            mm(b, banks[b][:, :], ones_bf[:], sq[:, kc, :], start=(kc == 0), stop=(kc == KC - 1),
               reads=[o_act[kc], o_const])
        P.op("act", lambda e, b=b: e.activation(out=rstd[:], in_=banks[b][:, :], func=AF.Sqrt, bias=epsD[:, 0:1], scale=1.0),
             reads=[o_bank[b], o_const], writes=[o_rstd])
        P.op("dve", lambda e: e.reciprocal(out=rstd[:], in_=rstd[:]), reads=[o_rstd], writes=[o_rstd])
        for kc in range(KC):
            tm, otm = (tmpA, o_tmpA) if kc % 2 == 0 else (tmpB, o_tmpB)
            P.op("dve", lambda e, kc=kc, tm=tm: e.tensor_tensor(out=tm[:], in0=hT[:, kc, :], in1=rstd[:], op=ALU.mult),
                 reads=[o_h[kc], o_rstd], writes=[otm])
            P.op("act", lambda e, kc=kc, tm=tm: e.activation(
                out=out_fn(kc), in_=tm[:], func=AF.Identity, scale=Amod[:, a_idx, kc:kc + 1], bias=sh_ap_fn(kc)),
                reads=[otm, o_mod, o_ada, o_const], writes=[out_objs[kc]])

    def ffn(l, which):
        s = 0 if which == 0 else 2
        wgu = ffn_gu[which]
        wd = ffn_d[which]
        norm_mod(l * 3 + s, lambda kc: ada[:, l, (3 * s) * KC + kc:(3 * s) * KC + kc + 1],
                 lambda kc: u[:, kc, :], o_u)
        nblk = (HC + 3) // 4
        for jb in range(nblk):
            nj = min(4, HC - jb * 4)
            wu, owu = load_w(wgu[l, :, DFF + jb * 512:DFF + jb * 512 + nj * 128], KC, nj * 128)
            wg, owg = load_w(wgu[l, :, jb * 512:jb * 512 + nj * 128], KC, nj * 128)
            bus = []
            for jl in range(nj):
                bu = next_bank(ALLB)
                bus.append(bu)
                for kc in range(KC):
                    mm(bu, banks[bu][:, :], wu[:, kc, jl * 128:(jl + 1) * 128], u[:, kc, :],
                       start=(kc == 0), stop=(kc == KC - 1), reads=[owu, o_u[kc]])
            for _ in range(4 - nj):
                next_bank(ALLB)
            for jl in range(nj):
                j = jb * 4 + jl
                bu = bus[jl]
                bg = next_bank(ALLB)
                for kc in range(KC):
                    mm(bg, banks[bg][:, :], wg[:, kc, jl * 128:(jl + 1) * 128], u[:, kc, :],
                       start=(kc == 0), stop=(kc == KC - 1), reads=[owg, o_u[kc]])
                tm, otm = (tmpA, o_tmpA) if j % 2 == 0 else (tmpB, o_tmpB)
                P.op("act", lambda e, bg=bg, tm=tm: e.activation(out=tm[:], in_=banks[bg][:, :], func=AF.Silu),
                     reads=[o_bank[bg]], writes=[otm])
                P.op("dve", lambda e, bu=bu, tm=tm, j=j: e.tensor_tensor(out=act[:, j, :], in0=banks[bu][:, :],
                                                                         in1=tm[:], op=ALU.mult),
                     reads=[o_bank[bu], otm], writes=[o_act[j]])
            for _ in range(4 - nj):
                next_bank(ALLB)
        parts = [(0, 16), (16, 16), (32, HC - 32)]
        for og in range(4):
            bs = [next_bank(ALLB) for _ in range(4)]
            for (k0, nk) in parts:
                wt, ow = load_w(wd[l, k0 * 128:(k0 + nk) * 128, og * 512:(og + 1) * 512], nk, 512)
                for o in range(4):
                    for kk in range(nk):
                        kc = k0 + kk
                        mm(bs[o], banks[bs[o]][:, :], wt[:, kk, o * 128:(o + 1) * 128], act[:, kc, :],
                           start=(kc == 0), stop=(kc == HC - 1), reads=[ow, o_act[kc]])
            for o in range(4):
                fc = og * 4 + o
                P.op("dve", lambda e, b=bs[o], fc=fc: e.scalar_tensor_tensor(
                    out=hT[:, fc, :], in0=banks[b][:, :], scalar=Gmod[:, l * 3 + s, fc:fc + 1], in1=hT[:, fc, :],
                    op0=ALU.mult, op1=ALU.add), reads=[o_bank[bs[o]], o_mod, o_h[fc]], writes=[o_h[fc]])

    Q0, AO0, MG0, CA0, PA0 = 0, 8, 16, 32, 36

    def proj_fm(l, wblk, owblk, jl, bank):
        for kc in range(KC):
            mm(bank, banks[bank][:, :], wblk[:, kc, jl * 128:(jl + 1) * 128], u[:, kc, :],
               start=(kc == 0), stop=(kc == KC - 1), reads=[owblk, o_u[kc]])

    def mixer(l, t):
        s = 1
        norm_mod(l * 3 + s, lambda kc: ada[:, l, (3 * s) * KC + kc:(3 * s) * KC + kc + 1],
                 lambda kc: u[:, kc, :], o_u)
        wa, owa = load_w(w_in[l, :, 0:512], KC, 512)
        wg, owg = load_w(w_in[l, :, 512:1024], KC, 512)
        for ch in range(4):
            if t == 0:
                P.op("dve", lambda e, ch=ch: e.memset(glu[:, ch, 0:HALO], 0.0), pwrites=[o_glu[ch]])
            else:
                P.op("dve", lambda e, ch=ch: e.tensor_copy(out=glu[:, ch, 0:HALO], in_=gluh[:, l, ch, :]),
                     reads=[o_gluh[l]], pwrites=[o_glu[ch]])
        bas = []
        for ch in range(4):
            ba = next_bank(ALLB)
            bas.append(ba)
            proj_fm(l, wa, owa, ch, ba)
        for ch in range(4):
            ba = bas[ch]
            bg = next_bank(ALLB)
            proj_fm(l, wg, owg, ch, bg)
            tm, otm = (tmpA, o_tmpA) if ch % 2 == 0 else (tmpB, o_tmpB)
            P.op("act", lambda e, bg=bg, tm=tm: e.activation(out=tm[:], in_=banks[bg][:, :], func=AF.Sigmoid),
                 reads=[o_bank[bg]], writes=[otm])
            P.op("dve", lambda e, ba=ba, ch=ch, tm=tm: e.tensor_tensor(out=glu[:, ch, HALO:HALO + T], in0=banks[ba][:, :],
                                                                       in1=tm[:], op=ALU.mult),
                 reads=[o_bank[ba], otm], pwrites=[o_glu[ch]])
        if t + 1 < NT:
            for ch in range(4):
                P.op("dve", lambda e, ch=ch: e.tensor_copy(out=gluh[:, l, ch, :], in_=glu[:, ch, T:T + HALO]),
                     reads=[o_glu[ch]], pwrites=[o_gluh[l]])
        for qb in range(2):
            wq, owq = load_w(w_in[l, :, 1024 + qb * 512:1024 + (qb + 1) * 512], KC, 512)
            for jl in range(4):
                c = qb * 4 + jl
                b = next_bank(ALLB)
                proj_fm(l, wq, owq, jl, b)
                P.op("act", lambda e, b=b, c=c: e.activation(out=act[:, Q0 + c, :], in_=banks[b][:, :],
                                                             func=AF.Copy, scale=0.125),
                     reads=[o_bank[b]], writes=[o_act[Q0 + c]])
        for kb_ in range(2):
            wk, owk = load_w(w_in[l, :, 2048 + kb_ * 512:2048 + (kb_ + 1) * 512], KC, 512)
            for jl in range(4):
                c = kb_ * 4 + jl
                b = next_bank(ALLB)
                proj_fm(l, wk, owk, jl, b)
                i = kst_ctr[0] % 2
                kst_ctr[0] += 1
                P.op("act", lambda e, b=b, i=i: e.activation(out=kst[i][:], in_=banks[b][:, :], func=AF.Copy),
                     reads=[o_bank[b]], writes=[o_kst[i]])
                P.op("sp", lambda e, i=i, c=c: e.dma_start(out=kcache[l][c * 128:(c + 1) * 128, t * T:(t + 1) * T],
                                                           in_=kst[i][:]),
                     reads=[o_kst[i]], pwrites=[o_kc[l]], dma="kst%d" % i)
        for vb in range(2):
            wv, owv = load_w(w_in[l, :, 3072 + vb * 512:3072 + (vb + 1) * 512], KC, 512)
            for sub in range(4):
                b = next_bank(ALLB)
                for kc in range(KC):
                    mm(b, banks[b][:, :], u[:, kc, sub * 128:(sub + 1) * 128], wv[:, kc, :],
                       start=(kc == 0), stop=(kc == KC - 1), reads=[owv, o_u[kc]])
                i = kst_ctr[0] % 2
                kst_ctr[0] += 1
                P.op("act", lambda e, b=b, i=i: e.activation(out=kst[i][:], in_=banks[b][:, :], func=AF.Copy),
                     reads=[o_bank[b]], writes=[o_kst[i]])
                kb = t * 4 + sub
                P.op("sp", lambda e, i=i, vb=vb, kb=kb: e.dma_start(
                    out=vcache[l][vb * 4:(vb + 1) * 4, :, kb, :].rearrange("c p f -> p c f"),
                    in_=kst[i][:].rearrange("p (c f) -> p c f", f=128)),
                    reads=[o_kst[i]], pwrites=[o_vc[l]], dma="kst%d" % i)
        wp, owp = load_w(w_in[l, :, 4096:4608], KC, 512)
        for gi in range(4):
            if t == 0:
                P.op("dve", lambda e, gi=gi: e.memset(pin[:, gi, 0:PH], 0.0), pwrites=[o_pin[gi]])
            else:
                P.op("dve", lambda e, gi=gi: e.tensor_copy(out=pin[:, gi, 0:PH], in_=pinh[:, l, gi, :]),
                     reads=[o_pinh[l]], pwrites=[o_pin[gi]])
            b = next_bank(ALLB)
            proj_fm(l, wp, owp, gi, b)
            P.op("act", lambda e, b=b, gi=gi: e.activation(out=pin[:, gi, PH:PH + T], in_=banks[b][:, :], func=AF.Copy),
                 reads=[o_bank[b]], pwrites=[o_pin[gi]])
            if t + 1 < NT:
                P.op("dve", lambda e, gi=gi: e.tensor_copy(out=pinh[:, l, gi, :], in_=pin[:, gi, T:T + PH]),
                     reads=[o_pin[gi]], pwrites=[o_pinh[l]])

        def cw(ch, j):
            o_ = off["convw"][0] + (l * 4 + ch) * CW + j
            return sm[:, o_:o_ + 1]

        conv_ops = []
        for ch in range(4):
            cb = smv("convb", l * 4 + ch, l * 4 + ch + 1)
            conv_ops.append(lambda ch=ch, cb=cb: P.op("dve", lambda e: e.tensor_scalar(
                out=cacc[:, ch, :], in0=glu[:, ch, 0:T], scalar1=cw(ch, 0), scalar2=cb, op0=ALU.mult, op1=ALU.add),
                reads=[o_glu[ch], o_sm], writes=[o_cacc[ch]]))
        for j in range(1, CW):
            for ch in range(4):
                conv_ops.append(lambda ch=ch, j=j: P.op("dve", lambda e: e.scalar_tensor_tensor(
                    out=cacc[:, ch, :], in0=glu[:, ch, j:j + T], scalar=cw(ch, j), in1=cacc[:, ch, :],
                    op0=ALU.mult, op1=ALU.add), reads=[o_glu[ch], o_sm, o_cacc[ch]], writes=[o_cacc[ch]]))

        def conv_ln():
            b1 = next_bank(ALLB)
            for ch in range(4):
                mm(b1, banks[b1][:, :], smv("ones"), cacc[:, ch, :], start=(ch == 0), stop=(ch == 3),
                   reads=[o_cacc[ch], o_sm])
            b2 = next_bank(ALLB)
            for ch in range(4):
                tm, otm = (tmpA, o_tmpA) if ch % 2 == 0 else (tmpB, o_tmpB)
                P.op("act", lambda e, ch=ch, tm=tm: e.activation(out=tm[:], in_=cacc[:, ch, :], func=AF.Square),
                     reads=[o_cacc[ch]], writes=[otm])
                mm(b2, banks[b2][:, :], smv("ones"), tm[:], start=(ch == 0), stop=(ch == 3), reads=[otm, o_sm])
            P.op("dve", lambda e: e.tensor_scalar(out=ebuf[:, 0:T], in0=banks[b1][:, :], scalar1=1.0 / DCONV, scalar2=None,
                                                  op0=ALU.mult), reads=[o_bank[b1]], writes=[o_ebuf])
            P.op("dve", lambda e: e.tensor_tensor(out=tmpA[:], in0=ebuf[:, 0:T], in1=ebuf[:, 0:T], op=ALU.mult),
                 reads=[o_ebuf], writes=[o_tmpA])
            P.op("dve", lambda e: e.scalar_tensor_tensor(out=rstd[:], in0=banks[b2][:, :], scalar=1.0 / DCONV, in1=tmpA[:],
                                                         op0=ALU.mult, op1=ALU.subtract),
                 reads=[o_bank[b2], o_tmpA], writes=[o_rstd])
            P.op("act", lambda e: e.activation(out=rstd[:], in_=rstd[:], func=AF.Sqrt, bias=eps1[:, 0:1], scale=1.0),
                 reads=[o_rstd, o_const], writes=[o_rstd])
            P.op("dve", lambda e: e.reciprocal(out=rstd[:], in_=rstd[:]), reads=[o_rstd], writes=[o_rstd])
            for ch in range(4):
                tm, otm = (tmpA, o_tmpA) if ch % 2 == 0 else (tmpB, o_tmpB)
                P.op("dve", lambda e, ch=ch, tm=tm: e.tensor_tensor(out=tm[:], in0=cacc[:, ch, :], in1=ebuf[:, 0:T],
                                                                    op=ALU.subtract),
                     reads=[o_cacc[ch], o_ebuf], writes=[otm])
                P.op("dve", lambda e, tm=tm: e.tensor_tensor(out=tm[:], in0=tm[:], in1=rstd[:], op=ALU.mult),
                     reads=[otm, o_rstd], writes=[otm])
                P.op("act", lambda e, ch=ch, tm=tm: e.activation(
                    out=act[:, CA0 + ch, :], in_=tm[:], func=AF.Silu,
                    scale=smv("lng", l * 4 + ch, l * 4 + ch + 1), bias=smv("lnb", l * 4 + ch, l * 4 + ch + 1)),
                    reads=[otm, o_sm], writes=[o_act[CA0 + ch]])

        W_ = PH + T
        for gi, win in enumerate(POOL_WINDOWS):
            x_ = pin[:, gi, :]
            P.op("dve", lambda e, x_=x_: e.tensor_tensor(out=ps2[:, 1:W_], in0=x_[:, 1:W_], in1=x_[:, 0:W_ - 1], op=ALU.add),
                 reads=[o_pin[gi]], writes=[o_ps2])
            cur, ocur, oth, ooth = ps2, o_ps2, ps4, o_ps4
            sh = 2
            lo = 1
            while sh < win:
                lo2 = lo + sh
                P.op("dve", lambda e, cur=cur, oth=oth, sh=sh, lo2=lo2: e.tensor_tensor(
                    out=oth[:, lo2:W_], in0=cur[:, lo2:W_], in1=cur[:, lo2 - sh:W_ - sh], op=ALU.add),
                    reads=[ocur], writes=[ooth])
                cur, ocur, oth, ooth = oth, ooth, cur, ocur
                lo = lo2
                sh *= 2
            if t == 0:
                P.op("dve", lambda e, cur=cur, gi=gi: e.tensor_tensor(
                    out=cur[:, PH:2 * PH], in0=cur[:, PH:2 * PH], in1=smv("pfix", gi * PH, (gi + 1) * PH), op=ALU.mult),
                    reads=[ocur, o_sm], pwrites=[ocur])
            P.op("dve", lambda e, cur=cur, x_=x_, win=win: e.scalar_tensor_tensor(
                out=wbf[0][:], in0=cur[:, PH:W_], scalar=1.0 / win, in1=x_[:, PH:W_], op0=ALU.mult, op1=ALU.subtract),
                reads=[ocur, o_pin[gi]], writes=[o_wbf[0]])
            b = next_bank(ALLB)
            o_ = (l * 4 + gi) * 128
            mm(b, banks[b][:, :], pw[:, o_:o_ + 128], wbf[0][:], start=True, stop=True, reads=[o_pw, o_wbf[0]])
            P.op("act", lambda e, b=b, gi=gi: e.activation(
                out=act[:, PA0 + gi, :], in_=banks[b][:, :], func=AF.Identity,
                scale=smv("pscale", l * 4 + gi, l * 4 + gi + 1), bias=0.0),
                reads=[o_bank[b], o_sm], writes=[o_act[PA0 + gi]])

        ZB = [0, 1, 2, 3, 4]
        defer_ada = (t == 0 and l + 1 < depth)
        nkb_tot = 4 * (t + 1)
        nseg = t + 1
        items = []
        for c in range(8):
            first = [True, True]
            for sg in reversed(range(nseg)):
                kb_lo = sg * 4
                for hh in range(2):
                    for kb in reversed(range(kb_lo, kb_lo + 4)):
                        items.append(dict(c=c, sg=sg, hh=hh, kb=kb, kl=kb - kb_lo, d=kb - 4 * t,
                                          first=first[hh], last=(kb == 0),
                                          newseg=(hh == 0 and kb == kb_lo + 3),
                                          endc=(sg == 0 and hh == 1 and kb == 0)))
                        first[hh] = False
        n_it = len(items)

        def load_seg(it):
            i = seg_ctr[0] % NSEG
            seg_ctr[0] += 1
            c, kb_lo = it["c"], it["sg"] * 4
            P.op("sp", lambda e: e.dma_start(out=ksA[i][0:64, :], in_=kcache[l][c * 128:c * 128 + 64, kb_lo * 128:(kb_lo + 4) * 128]),
                 reads=[o_kc[l]], pwrites=[o_ksA[i]], dma="ksA%d" % i)
            P.op("sp", lambda e: e.dma_start(out=ksB[i][64:128, :], in_=kcache[l][c * 128 + 64:c * 128 + 128, kb_lo * 128:(kb_lo + 4) * 128]),
                 reads=[o_kc[l]], pwrites=[o_ksB[i]], dma="ksB%d" % i)
            P.op("sp", lambda e: e.dma_start(out=vsA[i][:, :, 0:64], in_=vcache[l][c, :, kb_lo:kb_lo + 4, 0:64]),
                 reads=[o_vc[l]], pwrites=[o_vsA[i]], dma="vsA%d" % i)
            P.op("sp", lambda e: e.dma_start(out=vsB[i][:, :, 64:128], in_=vcache[l][c, :, kb_lo:kb_lo + 4, 64:128]),
                 reads=[o_vc[l]], pwrites=[o_vsB[i]], dma="vsB%d" % i)
            return i

        def stA(it, idx):
            if it["newseg"]:
                it["seg"] = load_seg(it)
            else:
                it["seg"] = items[idx - 1]["seg"]
            i, hh, kl, c = it["seg"], it["hh"], it["kl"], it["c"]
            zb = next_bank(ZB)
            it["zb"] = zb
            kt, okt = (ksA[i], o_ksA[i]) if hh == 0 else (ksB[i], o_ksB[i])
            mm(zb, banks[zb][:, :], kt[:, kl * 128:(kl + 1) * 128], act[:, Q0 + c, :], start=True, stop=False,
               reads=[okt, o_act[Q0 + c]])

        def stB(it, idx):
            zb, d = it["zb"], it["d"]
            si = idx % 2
            it["si"] = si
            sp_, osp = spb[si], o_sp[si]
            eb, oeb = (tmpA, o_tmpA) if si == 0 else (tmpB, o_tmpB)
            P.op("act", lambda e: e.activation(out=eb[:], in_=banks[zb][:, :], func=AF.Exp),
                 reads=[o_bank[zb]], writes=[oeb])

        def stB2(it, idx):
            zb, d = it["zb"], it["d"]
            si = it["si"]
            sp_, osp = spb[si], o_sp[si]
            eb, oeb = (tmpA, o_tmpA) if si == 0 else (tmpB, o_tmpB)
            P.op("act", lambda e: e.activation(out=sp_[:], in_=eb[:], func=AF.Ln, bias=1.0, scale=1.0),
                 reads=[oeb], writes=[osp])
            if d >= 0:
                mo = 384 - 128 * d
                P.op("dve", lambda e: e.tensor_tensor(out=sp_[:], in0=sp_[:].bitcast(F32), in1=maskb[:, mo:mo + T], op=ALU.mult),
                     reads=[osp, o_maskb], writes=[osp])

        def stC(it, idx):
            zb, hh, si = it["zb"], it["hh"], it["si"]
            sp_, osp = spb[si], o_sp[si]
            mm(zb, banks[zb][:, :], tri_r[:], sp_[:], start=False, stop=it["first"], reads=[osp, o_const])
            if not it["first"]:
                mm(zb, banks[zb][:, :], neg_r[:], Rb[hh][:], start=False, stop=True, reads=[o_R[hh], o_const])
            if not it["last"]:
                if it["first"]:
                    P.op("dve", lambda e: e.tensor_copy(out=Rb[hh][:], in_=sp_[:].bitcast(F32)),
                         reads=[osp], writes=[o_R[hh]])
                else:
                    P.op("dve", lambda e: e.tensor_tensor(out=Rb[hh][:], in0=Rb[hh][:].bitcast(F32), in1=sp_[:].bitcast(F32),
                                                          op=ALU.add), reads=[osp, o_R[hh]], writes=[o_R[hh]])

        def stD(it, idx):
            zb, d, si = it["zb"], it["d"], it["si"]
            wb_, owb = wbf[si], o_wbf[si]
            P.op("act", lambda e: e.activation(out=wb_[:], in_=banks[zb][:, :], func=AF.Exp),
                 reads=[o_bank[zb]], writes=[owb])
            if d >= 0:
                mo = 384 - 128 * d
                P.op("dve", lambda e: e.tensor_tensor(out=wb_[:], in0=wb_[:], in1=maskb[:, mo:mo + T], op=ALU.mult),
                     reads=[owb, o_maskb], writes=[owb])

        def stE(it, idx):
            i, hh, kl, c, si = it["seg"], it["hh"], it["kl"], it["c"], it["si"]
            avb = 6 + (c % 2)
            wb_, owb = wbf[si], o_wbf[si]
            vt, ovt = (vsA[i], o_vsA[i]) if hh == 0 else (vsB[i], o_vsB[i])
            mm(avb, banks[avb][:, :], vt[:, kl, :], wb_[:], start=(it["first"] and hh == 0), stop=(it["last"] and hh == 1),
               reads=[ovt, owb])
            if it["endc"]:
                P.op("act", lambda e: e.activation(out=act[:, AO0 + c, :], in_=banks[avb][:, :], func=AF.Copy),
                     reads=[o_bank[avb]], writes=[o_act[AO0 + c]])

        per_item = 2 if t == 0 else 1
        for s_ in range(n_it + 2):
            if s_ < n_it:
                stA(items[s_], s_)
                stB(items[s_], s_)
                stB2(items[s_], s_)
                if t == 0:
                    n_conv = 2
                elif items[s_]["d"] >= 0:
                    n_conv = 0
                else:
                    n_conv = 2 if t == 1 else 1
                for _ in range(n_conv):
                    if conv_ops:
                        conv_ops.pop(0)()
                if defer_ada and s_ < 36:
                    ada_block(l + 1, s_)
            if 0 <= s_ - 1 < n_it:
                stC(items[s_ - 1], s_ - 1)
                stD(items[s_ - 1], s_ - 1)
            if 0 <= s_ - 2 < n_it:
                stE(items[s_ - 2], s_ - 2)
        while conv_ops:
            conv_ops.pop(0)()
        if defer_ada:
            P.op("sp", lambda e: e.dma_start(out=rstd[:, 0:144], in_=bada_d[:, (l + 1) * 144:(l + 2) * 144]),
                 writes=[o_rstd], dma="bada2")
            ada_finish(l + 1, rstd[:, 0:144], [o_rstd])
        conv_ln()

        br_w = [(w_brc, 4, CA0), (w_bra, 8, AO0), (w_brp, 4, PA0)]
        for fg in range(4):
            for r in range(3):
                wgt, owgt = load_w(w_in[l, :, 4608 + r * D + fg * 512:4608 + r * D + (fg + 1) * 512], KC, 512)
                wsrc, nk, a0 = br_w[r]
                wbr, owbr = load_w(wsrc[l, :, fg * 512:(fg + 1) * 512], nk, 512)
                bgts = []
                for jl in range(4):
                    bgt = next_bank(ALLB)
                    bgts.append(bgt)
                    proj_fm(l, wgt, owgt, jl, bgt)
                for jl in range(4):
                    fc = fg * 4 + jl
                    bgt = bgts[jl]
                    by = next_bank(ALLB)
                    for kk in range(nk):
                        mm(by, banks[by][:, :], wbr[:, kk, jl * 128:(jl + 1) * 128], act[:, a0 + kk, :],
                           start=(kk == 0), stop=(kk == nk - 1), reads=[owbr, o_act[a0 + kk]])
                    P.op("act", lambda e, bgt=bgt: e.activation(out=tmpA[:], in_=banks[bgt][:, :], func=AF.Sigmoid),
                         reads=[o_bank[bgt]], writes=[o_tmpA])
                    if r == 0:
                        P.op("dve", lambda e, by=by, jl=jl: e.tensor_tensor(out=cacc[:, jl, :], in0=banks[by][:, :],
                                                                            in1=tmpA[:], op=ALU.mult),
                             reads=[o_bank[by], o_tmpA], writes=[o_cacc[jl]])
                    else:
                        P.op("dve", lambda e, by=by: e.tensor_tensor(out=tmpB[:], in0=banks[by][:, :], in1=tmpA[:],
                                                                     op=ALU.mult),
                             reads=[o_bank[by], o_tmpA], writes=[o_tmpB])
                        if r == 1:
                            P.op("dve", lambda e, jl=jl: e.tensor_tensor(out=cacc[:, jl, :], in0=cacc[:, jl, :],
                                                                         in1=tmpB[:], op=ALU.add),
                                 reads=[o_cacc[jl], o_tmpB], writes=[o_cacc[jl]])
                        else:
                            P.op("dve", lambda e, jl=jl, fc=fc: e.tensor_tensor(out=act[:, MG0 + fc, :], in0=cacc[:, jl, :],
                                                                                in1=tmpB[:], op=ALU.add),
                                 reads=[o_cacc[jl], o_tmpB], writes=[o_act[MG0 + fc]])
        for og in range(4):
            wo, owo = load_w(w_out[l, :, og * 512:(og + 1) * 512], KC, 512)
            for o in range(4):
                fc = og * 4 + o
                b = next_bank(ALLB)
                for kc in range(KC):
                    mm(b, banks[b][:, :], wo[:, kc, o * 128:(o + 1) * 128], act[:, MG0 + kc, :],
                       start=(kc == 0), stop=(kc == KC - 1), reads=[owo, o_act[MG0 + kc]])
                P.op("dve", lambda e, b=b, fc=fc: e.scalar_tensor_tensor(
                    out=hT[:, fc, :], in0=banks[b][:, :], scalar=Gmod[:, l * 3 + 1, fc:fc + 1], in1=hT[:, fc, :],
                    op0=ALU.mult, op1=ALU.add), reads=[o_bank[b], o_mod, o_h[fc]], writes=[o_h[fc]])

    for t in range(NT):
        for q4 in range(4):
            P.op("sp", lambda e, q4=q4, t=t: e.dma_start(
                out=hT[:, q4 * 4:(q4 + 1) * 4, :],
                in_=xT[q4 * 512:(q4 + 1) * 512, t * T:(t + 1) * T].rearrange("(kc p) s -> p kc s", p=128)),
                writes=o_h[q4 * 4:(q4 + 1) * 4], dma="xin%d" % q4)
        for l in range(depth):
            ffn(l, 0)
            mixer(l, t)
            ffn(l, 1)

        def out_fn(kc):
            return ost[kc % 2][:]

        sq = act
        for kc in range(KC):
            if kc % 2 == 0:
                P.op("act", lambda e, kc=kc: e.activation(out=sq[:, kc, :], in_=hT[:, kc, :], func=AF.Square),
                     reads=[o_h[kc]], writes=[o_act[kc]])
            else:
                P.op("dve", lambda e, kc=kc: e.tensor_tensor(out=sq[:, kc, :], in0=hT[:, kc, :], in1=hT[:, kc, :], op=ALU.mult),
                     reads=[o_h[kc]], writes=[o_act[kc]])
        b = next_bank(ALLB)
        for kc in range(KC):
            mm(b, banks[b][:, :], ones_bf[:], sq[:, kc, :], start=(kc == 0), stop=(kc == KC - 1),
               reads=[o_act[kc], o_const])
        P.op("act", lambda e, b=b: e.activation(out=rstd[:], in_=banks[b][:, :], func=AF.Sqrt, bias=epsD[:, 0:1], scale=1.0),
             reads=[o_bank[b], o_const], writes=[o_rstd])
        P.op("dve", lambda e: e.reciprocal(out=rstd[:], in_=rstd[:]), reads=[o_rstd], writes=[o_rstd])
        for kc in range(KC):
            tm, otm = (tmpA, o_tmpA) if kc % 2 == 0 else (tmpB, o_tmpB)
            P.op("dve", lambda e, kc=kc, tm=tm: e.tensor_tensor(out=tm[:], in0=hT[:, kc, :], in1=rstd[:], op=ALU.mult),
                 reads=[o_h[kc], o_rstd], writes=[otm])
            P.op("act", lambda e, kc=kc, tm=tm: e.activation(
                out=tm[:], in_=tm[:], func=AF.Identity, scale=Amod[:, depth * 3, kc:kc + 1], bias=0.0),
                reads=[otm, o_mod, o_const], writes=[otm])
            P.op("sp", lambda e, kc=kc, tm=tm, t=t: e.dma_start(out=yT[kc * 128:(kc + 1) * 128, t * T:(t + 1) * T],
                                                               in_=tm[:]),
                 reads=[otm], pwrites=[o_y], dma="ost%d" % (kc % 2))
    P.final_wait("sp", [o_y])
    P.emit()
    for g in reversed(guards):
        g.__exit__(None, None, None)
    return nc, P


_CACHE = {}


def make_in_maps(inputs, S, depth, batch_ids):
    f = lambda k: np.ascontiguousarray(np.asarray(inputs[k], dtype=np.float32))
    shared = {
        "w_ada": f("w_ada"), "ffn1_w_gu": f("ffn1_w_gu"), "ffn2_w_gu": f("ffn2_w_gu"),
        "ffn1_w_d": f("ffn1_w_d"), "ffn2_w_d": f("ffn2_w_d"), "w_in": f("w_in"),
        "w_br_conv": f("w_br_conv"), "w_br_attn": f("w_br_attn"), "w_br_pool": f("w_br_pool"), "w_out": f("w_out"),
    }
    pw = f("pool_w")
    shared["pool_wT"] = np.ascontiguousarray(pw.transpose(2, 0, 1, 3).reshape(128, depth * 4 * 128))
    x = f("x")
    c = f("c")
    maps = []
    for b in batch_ids:
        m = dict(shared)
        m["xT"] = np.ascontiguousarray(x[b].T)
        m["bada"] = np.ascontiguousarray(f("b_ada").reshape(depth * 144, 128).T)
        kk_ = np.arange(128)
        jj_ = np.arange(896)
        m["maskf"] = (kk_[:, None] < (jj_[None, :] - 384)).astype(np.float32)
        m["small"] = pack_small(depth, c[b], f("norm_g"), f("b_ada"), f("conv_w"), f("conv_b"), f("conv_ln_g"),
                                f("conv_ln_b"), f("pool_scale"), f("final_g"))
        maps.append(m)
    return maps


def run(inputs, S, depth, trace=False):
    B = np.asarray(inputs["x"]).shape[0]
    key = (S, depth)
    if key not in _CACHE:
        _CACHE[key] = build_program(S, depth)[0]
    nc = _CACHE[key]
    maps = make_in_maps(inputs, S, depth, list(range(B)))
    res = run_bass_kernel_spmd(nc, maps, core_ids=list(range(B)), trace=trace)
    out = np.stack([np.ascontiguousarray(r["yT"].T) for r in res.results], axis=0).astype(np.float32)
    return out, res


def kernel(**inputs):
    out, _ = run(inputs, 4096, 4)
    return out
```
